# Optimizing a Trainium2 kernel written in Bass

```python
import jax, jax.numpy as jnp
from jax import lax
import numpy as np

D_MODEL = 2048
BATCH = 4
SEQ = 2048
DEPTH = 1
DEC_BATCH = 32
DEC_SEQ = 1
PAST_LEN = 16384
PAGE_SIZE = 128

N_MLSTM_HEADS = 4
MLSTM_DK = D_MODEL // 8
MLSTM_DV = D_MODEL // 4
MLSTM_QK = N_MLSTM_HEADS * MLSTM_DK
MLSTM_V = N_MLSTM_HEADS * MLSTM_DV
MLSTM_CHUNK = 64
HEAD_DIM = 64
N_Q_HEADS = D_MODEL // HEAD_DIM
N_KV_HEADS = 4
GROUP = N_Q_HEADS // N_KV_HEADS
ATT_Q = N_Q_HEADS * HEAD_DIM
ATT_KV = N_KV_HEADS * HEAD_DIM
WINDOW = 128
ROT_DIM = HEAD_DIM // 4
ROPE_THETA = 500000.0
D_FF = 11 * D_MODEL // 4
CONV_WIDTH = 3
EPS = 1e-6
D_IN = 2 * MLSTM_QK + 2 * MLSTM_V + 2 * N_MLSTM_HEADS + ATT_Q + 2 * ATT_KV + 2 * D_MODEL

kernel_name = "hybrid_mlstm_swa_sink_convffn_step"


def _rmsnorm(x, g):
    x32 = x.astype(jnp.float32)
    y = x32 * lax.rsqrt(jnp.mean(x32 * x32, axis=-1, keepdims=True) + EPS) * g.astype(jnp.float32)
    return y.astype(x.dtype)


def _rope(x, pos):
    half = ROT_DIM // 2
    inv = ROPE_THETA ** (-jnp.arange(half, dtype=jnp.float32) * 2.0 / ROT_DIM)
    ang = pos.astype(jnp.float32)[:, None] * inv[None, :]
    cos = jnp.cos(ang)[:, None, :]
    sin = jnp.sin(ang)[:, None, :]
    x32 = x.astype(jnp.float32)
    x1, x2, rest = x32[..., :half], x32[..., half:ROT_DIM], x32[..., ROT_DIM:]
    out = jnp.concatenate([x1 * cos - x2 * sin, x2 * cos + x1 * sin, rest], axis=-1)
    return out.astype(x.dtype)


def _mlstm_chunk(carry, xs):
    C, n, m = carry
    q, k, v, ig, lf = xs
    L = q.shape[2]
    b = jnp.cumsum(lf, axis=-1)
    causal = jnp.tril(jnp.ones((L, L), dtype=bool))
    D = jnp.where(causal, b[..., :, None] - b[..., None, :] + ig[..., None, :], -jnp.inf)
    inter = b + m[..., None]
    m_t = jnp.maximum(jnp.max(D, axis=-1), inter)
    S = jnp.einsum("bhlk,bhsk->bhls", q, k) * jnp.exp(D - m_t[..., None])
    w_inter = jnp.exp(inter - m_t)
    num = jnp.einsum("bhls,bhsv->bhlv", S, v) + w_inter[..., None] * jnp.einsum("bhvk,bhlk->bhlv", C, q)
    den = jnp.sum(S, axis=-1) + w_inter * jnp.einsum("bhk,bhlk->bhl", n, q)
    h = num / jnp.maximum(jnp.abs(den), jnp.exp(-m_t))[..., None]
    m_new = m_t[..., -1]
    w_last = jnp.exp(b[..., -1:] - b + ig - m_new[..., None])
    decay = jnp.exp(b[..., -1] + m - m_new)
    C_new = decay[..., None, None] * C + jnp.einsum("bhl,bhlv,bhlk->bhvk", w_last, v, k)
    n_new = decay[..., None] * n + jnp.einsum("bhl,bhlk->bhk", w_last, k)
    return (C_new, n_new, m_new), h


def _mlstm(q, k, v, ig, lf, C0, n0, m0, chunk):
    B, T = q.shape[:2]
    nc = T // chunk

    def to_chunks(a):
        a = a.reshape((B, nc, chunk) + a.shape[2:])
        a = jnp.moveaxis(a, 3, 2)
        return jnp.moveaxis(a, 1, 0)

    xs = (to_chunks(q), to_chunks(k), to_chunks(v), to_chunks(ig), to_chunks(lf))
    (C, n, m), h = lax.scan(_mlstm_chunk, (C0, n0, m0), xs)
    h = jnp.moveaxis(h, 0, 1)
    h = jnp.moveaxis(h, 2, 3).reshape(B, T, N_MLSTM_HEADS, MLSTM_DV)
    return h, C, n, m


def _sink_attention(q, k, v, mask, sinks):
    s = jnp.einsum("ntkgd,nskd->nkgts", q, k).astype(jnp.float32) * (HEAD_DIM ** -0.5)
    s = jnp.where(mask, s, -jnp.inf)
    sink = jnp.broadcast_to(sinks.astype(jnp.float32)[None, :, :, None, None], s.shape[:-1] + (1,))
    p = jax.nn.softmax(jnp.concatenate([s, sink], axis=-1), axis=-1)[..., :-1]
    return jnp.einsum("nkgts,nskd->ntkgd", p.astype(v.dtype), v)


def _swa_prompt(q, k, v, sinks):
    B, T = q.shape[:2]
    nb = T // WINDOW
    qb = q.reshape(B * nb, WINDOW, N_KV_HEADS, GROUP, HEAD_DIM)

    def band(a):
        ab = a.reshape(B, nb, WINDOW, N_KV_HEADS, HEAD_DIM)
        prev = jnp.concatenate([jnp.zeros_like(ab[:, :1]), ab[:, :-1]], axis=1)
        return jnp.concatenate([prev, ab], axis=2).reshape(B * nb, 2 * WINDOW, N_KV_HEADS, HEAD_DIM)

    a_idx = jnp.arange(WINDOW)[:, None]
    s_idx = jnp.arange(2 * WINDOW)[None, :]
    in_band = (s_idx >= a_idx) & (s_idx <= a_idx + WINDOW)
    blk = jnp.arange(nb)[:, None, None]
    valid = in_band[None] & ((blk > 0) | (s_idx[None] >= WINDOW))
    mask = jnp.broadcast_to(valid[None], (B, nb, WINDOW, 2 * WINDOW)).reshape(B * nb, 1, 1, WINDOW, 2 * WINDOW)
    o = _sink_attention(qb, band(k), band(v), mask, sinks)
    return o.reshape(B, T, ATT_Q)


def _swa_sample(q, k, v, k_past, v_past, pos, sinks):
    B, T = q.shape[:2]
    kk = jnp.concatenate([k_past.astype(k.dtype), k], axis=1)
    vv = jnp.concatenate([v_past.astype(v.dtype), v], axis=1)
    kpos = pos[0] - WINDOW + jnp.arange(WINDOW + T)
    diff = pos[:, None] - kpos[None, :]
    mask = ((diff >= 0) & (diff <= WINDOW))[None, None, None]
    o = _sink_attention(q.reshape(B, T, N_KV_HEADS, GROUP, HEAD_DIM), kk, vv, mask, sinks)
    return o.reshape(B, T, ATT_Q), kk[:, T:], vv[:, T:]


def _conv_ffn(h, conv_past, w_up, w_conv, b_conv, w_down):
    u = h @ w_up
    T = u.shape[1]
    uu = jnp.concatenate([conv_past.astype(u.dtype), u], axis=1)
    c = b_conv
    for j in range(CONV_WIDTH):
        c = c + uu[:, j:j + T] * w_conv[j]
    g, val = jnp.split(c, 2, axis=-1)
    y = jax.nn.gelu(g, approximate=True) * val
    return y @ w_down, uu[:, -(CONV_WIDTH - 1):]


def _layer(x, pos, C0, n0, m0, k_past, v_past, conv_past, chunk,
           g_pre_mix, w_in, b_if, sinks, w_branch_a, w_branch_b, w_out, g_post_mix,
           g_pre_ffn, w_up, w_conv, b_conv, w_down, g_post_ffn):
    B, T, _ = x.shape
    f32 = jnp.float32
    sizes = [MLSTM_QK, MLSTM_QK, MLSTM_V, MLSTM_V, N_MLSTM_HEADS, N_MLSTM_HEADS,
             ATT_Q, ATT_KV, ATT_KV, D_MODEL, D_MODEL]
    idx = [int(i) for i in np.cumsum(sizes)[:-1]]
    h = _rmsnorm(x, g_pre_mix)
    z = h @ w_in
    q_m, k_m, v_m, o_m, ig, fg, q_a, k_a, v_a, gate_a, gate_b = jnp.split(z, idx, axis=-1)

    ifb = b_if.astype(f32)
    qm = q_m.reshape(B, T, N_MLSTM_HEADS, MLSTM_DK).astype(f32)
    km = k_m.reshape(B, T, N_MLSTM_HEADS, MLSTM_DK).astype(f32) * (MLSTM_DK ** -0.5)
    vm = v_m.reshape(B, T, N_MLSTM_HEADS, MLSTM_DV).astype(f32)
    igt = ig.astype(f32) + ifb[:N_MLSTM_HEADS]
    lft = jax.nn.log_sigmoid(fg.astype(f32) + ifb[N_MLSTM_HEADS:])
    hm, C, n, m = _mlstm(qm, km, vm, igt, lft, C0.astype(f32), n0.astype(f32), m0.astype(f32), chunk)
    hm = (jax.nn.sigmoid(o_m.astype(f32)) * hm.reshape(B, T, MLSTM_V)).astype(x.dtype)

    qa = _rope(q_a.reshape(B, T, N_Q_HEADS, HEAD_DIM), pos)
    ka = _rope(k_a.reshape(B, T, N_KV_HEADS, HEAD_DIM), pos)
    va = v_a.reshape(B, T, N_KV_HEADS, HEAD_DIM)
    snk = sinks.reshape(N_KV_HEADS, GROUP)
    if k_past is None:
        ha = _swa_prompt(qa, ka, va, snk)
        new_k, new_v = ka[:, T - WINDOW:], va[:, T - WINDOW:]
    else:
        ha, new_k, new_v = _swa_sample(qa, ka, va, k_past, v_past, pos, snk)

    mix = (jax.nn.sigmoid(gate_a) * (hm @ w_branch_a) + jax.nn.sigmoid(gate_b) * (ha @ w_branch_b)) @ w_out
    x = x + _rmsnorm(mix, g_post_mix)

    f, conv_new = _conv_ffn(_rmsnorm(x, g_pre_ffn), conv_past, w_up, w_conv, b_conv, w_down)
    x = x + _rmsnorm(f, g_post_ffn)
    dt = x.dtype
    return x, C.astype(dt), n.astype(dt), m.astype(dt), new_k, new_v, conv_new


def setup_inputs(seed: int = 0) -> dict:
    key = jax.random.key(seed)
    ks = jax.random.split(key, 24)
    f32 = jnp.float32
    L = DEPTH

    def nrm(k, shape, s):
        return jax.random.normal(k, shape, f32) * s

    return {
        "x_prompt": nrm(ks[0], (BATCH, SEQ, D_MODEL), 1.0),
        "x_sample": nrm(ks[1], (DEC_BATCH, DEC_SEQ, D_MODEL), 1.0),
        "state_mlstm_c": nrm(ks[2], (L, DEC_BATCH, N_MLSTM_HEADS, MLSTM_DV, MLSTM_DK), 0.1),
        "state_mlstm_n": nrm(ks[3], (L, DEC_BATCH, N_MLSTM_HEADS, MLSTM_DK), 0.1),
        "state_mlstm_m": nrm(ks[4], (L, DEC_BATCH, N_MLSTM_HEADS), 0.5),
        "cache_swa_k": nrm(ks[5], (L, DEC_BATCH, WINDOW, N_KV_HEADS, HEAD_DIM), 1.0),
        "cache_swa_v": nrm(ks[6], (L, DEC_BATCH, WINDOW, N_KV_HEADS, HEAD_DIM), 1.0),
        "state_ffn_conv": nrm(ks[7], (L, DEC_BATCH, CONV_WIDTH - 1, 2 * D_FF), 1.0),
        "g_pre_mix": 1.0 + nrm(ks[8], (L, D_MODEL), 0.05),
        "w_in": nrm(ks[9], (L, D_MODEL, D_IN), D_MODEL ** -0.5),
        "b_if": jnp.concatenate([-1.0 + nrm(ks[10], (L, N_MLSTM_HEADS), 0.1),
                                 3.0 + nrm(ks[11], (L, N_MLSTM_HEADS), 0.1)], axis=-1),
        "attn_sinks": nrm(ks[12], (L, N_Q_HEADS), 0.5),
        "w_branch_a": nrm(ks[13], (L, MLSTM_V, D_MODEL), MLSTM_V ** -0.5),
        "w_branch_b": nrm(ks[14], (L, ATT_Q, D_MODEL), ATT_Q ** -0.5),
        "w_out": nrm(ks[15], (L, D_MODEL, D_MODEL), D_MODEL ** -0.5),
        "g_post_mix": 1.0 + nrm(ks[16], (L, D_MODEL), 0.05),
        "g_pre_ffn": 1.0 + nrm(ks[17], (L, D_MODEL), 0.05),
        "w_up": nrm(ks[18], (L, D_MODEL, 2 * D_FF), D_MODEL ** -0.5),
        "w_conv": nrm(ks[19], (L, CONV_WIDTH, 2 * D_FF), CONV_WIDTH ** -0.5),
        "b_conv": nrm(ks[20], (L, 2 * D_FF), 0.02),
        "w_down": nrm(ks[21], (L, D_FF, D_MODEL), D_FF ** -0.5),
        "g_post_ffn": 1.0 + nrm(ks[22], (L, D_MODEL), 0.05),
    }


def reference(x_prompt, x_sample, state_mlstm_c, state_mlstm_n, state_mlstm_m,
              cache_swa_k, cache_swa_v, state_ffn_conv,
              g_pre_mix, w_in, b_if, attn_sinks, w_branch_a, w_branch_b, w_out, g_post_mix,
              g_pre_ffn, w_up, w_conv, b_conv, w_down, g_post_ffn):
    Bp, Tp, _ = x_prompt.shape
    Bs, Ts, _ = x_sample.shape
    pos_p = jnp.arange(Tp)
    pos_s = PAST_LEN + jnp.arange(Ts)
    dt = x_prompt.dtype
    yp, ys = x_prompt, x_sample
    cp, np_, mp, kp, vp, fp = [], [], [], [], [], []
    cs, ns, ms, kss, vs, fs = [], [], [], [], [], []
    for l in range(DEPTH):
        lw = (g_pre_mix[l], w_in[l], b_if[l], attn_sinks[l], w_branch_a[l], w_branch_b[l], w_out[l],
              g_post_mix[l], g_pre_ffn[l], w_up[l], w_conv[l], b_conv[l], w_down[l], g_post_ffn[l])
        C0 = jnp.zeros((Bp, N_MLSTM_HEADS, MLSTM_DV, MLSTM_DK), dt)
        n0 = jnp.zeros((Bp, N_MLSTM_HEADS, MLSTM_DK), dt)
        m0 = jnp.zeros((Bp, N_MLSTM_HEADS), dt)
        conv0 = jnp.zeros((Bp, CONV_WIDTH - 1, 2 * D_FF), dt)
        yp, c_, n_, m_, k_, v_, f_ = _layer(yp, pos_p, C0, n0, m0, None, None, conv0, MLSTM_CHUNK, *lw)
        cp.append(c_); np_.append(n_); mp.append(m_); kp.append(k_); vp.append(v_); fp.append(f_)
        ys, c_, n_, m_, k_, v_, f_ = _layer(ys, pos_s, state_mlstm_c[l], state_mlstm_n[l], state_mlstm_m[l],
                                            cache_swa_k[l], cache_swa_v[l], state_ffn_conv[l], Ts, *lw)
        cs.append(c_); ns.append(n_); ms.append(m_); kss.append(k_); vs.append(v_); fs.append(f_)
    return (yp, ys,
            jnp.stack(cp), jnp.stack(np_), jnp.stack(mp), jnp.stack(kp), jnp.stack(vp), jnp.stack(fp),
            jnp.stack(cs), jnp.stack(ns), jnp.stack(ms), jnp.stack(kss), jnp.stack(vs), jnp.stack(fs))
```

```python
import os
import numpy as np
from contextlib import ExitStack
import concourse.bass as bass
import concourse.mybir as mybir
from concourse.bass_utils import run_bass_kernel_spmd

F32 = mybir.dt.float32
BF16 = mybir.dt.bfloat16
AF = mybir.ActivationFunctionType
ALU = mybir.AluOpType
AX = mybir.AxisListType

EPS = 1e-6
BIG = 1.0e4
D = 2048
DIN = 12808
DFF = 5632
ZC = dict(qm=0, km=1024, vm=2048, om=4096, ig=6144, fg=6148, qa=6152, ka=8200, va=8456, ga=8712, gb=10760)
NFULL = 1152
NFS = NFULL + 4
ENGS = ("pe", "act", "dve", "pool", "sp")


class Sched:
    def __init__(self, nc, n_dma_sems=44, n_pool=12):
        self.nc = nc
        self.q = {e: [] for e in ENGS}
        self.sem = {}
        self.cnt = {e: 0 for e in ENGS}
        self.seen = {e: {} for e in ENGS}
        self.lastw = {}
        self.reads = {}
        self.dma_sems = []
        self.dma_cnt = []
        self.dma_rr = {}
        self.n_dma_sems = n_dma_sems
        self.n_pool = n_pool

    def setup(self, stack):
        nc = self.nc
        for e in ENGS:
            if e != "sp":
                self.sem[e] = stack.enter_context(nc.semaphore("s_" + e))
        for i in range(self.n_dma_sems):
            self.dma_sems.append(stack.enter_context(nc.semaphore("s_dma%d" % i)))
            self.dma_cnt.append(0)

    def _deps(self, eng, reads, writes):
        evs = []
        for r in reads:
            if r in self.lastw:
                evs.append(self.lastw[r])
        for w in writes:
            if w in self.lastw:
                evs.append(self.lastw[w])
            evs.extend(self.reads.get(w, []))
        need = {}
        for (sem, val, src) in evs:
            if src == "pe" and eng == "pe":
                continue
            k = id(sem)
            if self.seen[eng].get(k, 0) >= val:
                continue
            if k not in need or need[k][1] < val:
                need[k] = (sem, val)
        for k, (sem, val) in need.items():
            self.seen[eng][k] = val
        return list(need.values())

    def _commit(self, ev, reads, writes):
        for r in reads:
            self.reads.setdefault(r, []).append(ev)
        for w in writes:
            self.lastw[w] = ev
            self.reads[w] = []

    def _emit_waits(self, eng, waits):
        for (sem, val) in waits:
            self.q[eng].append(lambda e, sem=sem, val=val: e.wait_ge(sem, val))

    def op(self, eng, fn, reads=(), writes=()):
        self._emit_waits(eng, self._deps(eng, reads, writes))
        self.cnt[eng] += 1
        sem = self.sem[eng]
        self.q[eng].append(lambda e, fn=fn, sem=sem: fn(e).then_inc(sem, 1))
        ev = (sem, self.cnt[eng], eng)
        self._commit(ev, reads, writes)
        return ev

    def pe_group(self, fns, reads=(), writes=()):
        eng = "pe"
        self._emit_waits(eng, self._deps(eng, reads, writes))
        self.cnt[eng] += 1
        sem = self.sem[eng]
        for fn in fns[:-1]:
            self.q[eng].append(lambda e, fn=fn: fn(e))
        fn = fns[-1]
        self.q[eng].append(lambda e, fn=fn, sem=sem: fn(e).then_inc(sem, 1))
        ev = (sem, self.cnt[eng], eng)
        self._commit(ev, reads, writes)
        return ev

    def dma(self, eng, out, in_, reads=(), writes=(), **kw):
        self._emit_waits(eng, self._deps(eng, reads, writes))
        lo, hi = (0, self.n_pool) if eng == "pool" else (self.n_pool, self.n_dma_sems)
        i = self.dma_rr.get(eng, lo)
        self.dma_rr[eng] = lo + (i + 1 - lo) % (hi - lo)
        sem = self.dma_sems[i]
        prev = self.dma_cnt[i]
        if prev > 0 and self.seen[eng].get(id(sem), 0) < prev:
            self.q[eng].append(lambda e, sem=sem, prev=prev: e.wait_ge(sem, prev))
            self.seen[eng][id(sem)] = prev
        self.dma_cnt[i] += 16
        val = self.dma_cnt[i]
        self.q[eng].append(
            lambda e, out=out, in_=in_, sem=sem, kw=kw: e.dma_start(out=out, in_=in_, **kw).then_inc(sem, 16))
        ev = (sem, val, "dma")
        self._commit(ev, reads, writes)
        return ev

    def flush(self):
        nc = self.nc
        for i, sem in enumerate(self.dma_sems):
            if self.dma_cnt[i] > 0:
                self.q["sp"].append(lambda e, sem=sem, val=self.dma_cnt[i]: e.wait_ge(sem, val))
        for en in ENGS:
            if en != "sp" and self.cnt[en] > 0:
                self.q["sp"].append(lambda e, sem=self.sem[en], val=self.cnt[en]: e.wait_ge(sem, val))
        q = self.q
        self.q = {e: [] for e in ENGS}
        with nc.Block() as block:
            @block.tensor
            def _(eng):
                for f in q["pe"]:
                    f(eng)

            @block.scalar
            def _(eng):
                for f in q["act"]:
                    f(eng)

            @block.vector
            def _(eng):
                for f in q["dve"]:
                    f(eng)

            @block.gpsimd
            def _(eng):
                for f in q["pool"]:
                    f(eng)

            @block.sync
            def _(eng):
                for f in q["sp"]:
                    f(eng)
        for e in ENGS:
            for en in ENGS:
                if en != "sp":
                    self.seen[e][id(self.sem[en])] = self.cnt[en]
            for i, sem in enumerate(self.dma_sems):
                self.seen[e][id(sem)] = self.dma_cnt[i]
        self.lastw = {}
        self.reads = {}


class Ring:
    def __init__(self, st, nc, name, n, shape, dt, psum=False):
        mk = nc.psum_tensor if psum else nc.sbuf_tensor
        self.t = [st.enter_context(mk("r_%s%d" % (name, i), shape, dt)) for i in range(n)]
        self.k = ["%s%d" % (name, i) for i in range(n)]
        self.i = 0

    def next(self):
        i = self.i
        self.i = (i + 1) % len(self.t)
        return self.t[i], self.k[i]


class RingCat:
    def __init__(self, *rings):
        self.t = [t for r in rings for t in r.t]
        self.k = [k for r in rings for k in r.k]
        self.i = 0

    next = Ring.next


def build(nstages=99, debug=False):
    nc = bass.Bass("TRN2", target_bir_lowering=False)
    s = Sched(nc)

    def din(name, shape):
        return nc.dram_tensor(name, list(shape), F32, kind="ExternalInput").ap()

    def dout(name, shape):
        return nc.dram_tensor(name, list(shape), F32, kind="ExternalOutput").ap()

    def dscr(name, shape, dt):
        return nc.dram_tensor(name, list(shape), dt, kind="ExternalOutput" if debug else "Internal").ap()

    xw = din("xw", [2048, D]); xs = din("xs", [4, D]); gvalid = din("gvalid", [4, 2048])
    masks = din("masks", [2, 128, 256]); ropet = din("ropet", [2052, 16])
    identf_d = din("identf", [128, 128]); maskT_d = din("maskT", [64, 64])
    st_c = din("st_c", [4, 4, 512, 256]); st_n = din("st_n", [4, 4, 256]); st_m = din("st_m", [4, 4])
    ck = din("ck", [4, 128, 256]); cv = din("cv", [4, 128, 256]); cst = din("cst", [8, 11264])
    g_pre = din("g_pre", [1, D]); w_in = din("w_in", [D, DIN]); b_if = din("b_if", [1, 8])
    sinks = din("sinks", [1, 32]); w_a = din("w_a", [D, D]); w_b = din("w_b", [D, D]); w_o = din("w_o", [D, D])
    g_pm = din("g_pm", [1, D]); g_pf = din("g_pf", [1, D]); w_up = din("w_up", [D, 2 * DFF])
    w_cv = din("w_cv", [3, 2 * DFF]); b_cv = din("b_cv", [1, 2 * DFF]); w_dn = din("w_dn", [DFF, D])
    g_po = din("g_po", [1, D])
    y_o = dout("y", [1024, D]); ys_o = dout("ys", [4, D])
    c_p = dout("c_p", [4, 512, 256]); n_p = dout("n_p", [4, 256]); m_p = dout("m_p", [1, 4])
    k_p = dout("k_p", [128, 256]); v_p = dout("v_p", [128, 256]); conv_p = dout("conv_p", [2, 2 * DFF])
    c_s = dout("c_s", [4, 4, 512, 256]); n_s = dout("n_s", [4, 4, 256]); m_s = dout("m_s", [4, 4])
    k_s = dout("k_s", [4, 128, 256]); v_s = dout("v_s", [4, 128, 256]); conv_s = dout("conv_s", [4, 2, 2 * DFF])
    zk = dscr("zk", [2048, 1024], BF16); zv = dscr("zv", [2048, 2048], BF16)
    zo = dscr("zo", [NFULL, 2048], BF16)
    zqT = dscr("zqT", [1024, NFULL], BF16); zkT = dscr("zkT", [1024, NFULL], BF16)
    zs = dscr("zs", [4, DIN], F32)
    qaT = dscr("qaT", [2048, NFULL], BF16); kaT = dscr("kaT", [256, 1280], BF16); va = dscr("va", [1280, 256], BF16)
    sga = dscr("sga", [2048, NFS], BF16); sgb = dscr("sgb", [2048, NFS], BF16)
    hm = dscr("hm", [NFULL, 2048], BF16); ha = dscr("ha", [NFS, 2048], BF16)
    mo = dscr("mo", [NFS, 2048], F32); x1 = dscr("x1", [NFS, 2048], F32)
    ysc = dscr("ysc", [DFF, 1028], BF16); fsc = dscr("fsc", [1028, 2048], F32)
    dsc = dscr("dsc", [4, 32], F32); sscr = dscr("sscr", [4, 12], F32); vscr = dscr("vscr", [4, 256], BF16)

    def ACT(out, in_, func, r, w, **kw):
        s.op("act", lambda e: e.activation(out=out, in_=in_, func=func, **kw), r, w)

    def TT(out, in0, in1, op, r, w, eng="dve"):
        s.op(eng, lambda e: e.tensor_tensor(out=out, in0=in0, in1=in1, op=op), r, w)

    def TS(out, in0, s1, op0, r, w, s2=None, op1=None, eng="dve"):
        if op1 is None:
            s.op(eng, lambda e: e.tensor_scalar(out=out, in0=in0, scalar1=s1, scalar2=None, op0=op0), r, w)
        else:
            s.op(eng, lambda e: e.tensor_scalar(out=out, in0=in0, scalar1=s1, scalar2=s2, op0=op0, op1=op1), r, w)

    def STT(out, in0, sc, in1, op0, op1, r, w, accum=None):
        if accum is None:
            s.op("dve", lambda e: e.scalar_tensor_tensor(out=out, in0=in0, scalar=sc, in1=in1, op0=op0, op1=op1), r, w)
        else:
            s.op("dve", lambda e: e.scalar_tensor_tensor(out=out, in0=in0, scalar=sc, in1=in1, op0=op0, op1=op1,
                                                         accum_out=accum), r, w)

    def CP(eng, out, in_, r, w):
        if eng == "act":
            s.op("act", lambda e: e.copy(out=out, in_=in_), r, w)
        else:
            s.op(eng, lambda e: e.tensor_copy(out=out, in_=in_), r, w)

    def RECIP(out, in_, r, w):
        s.op("dve", lambda e: e.reciprocal(out=out, in_=in_), r, w)

    def MEMSET(ap, val, w, eng="dve"):
        s.op(eng, lambda e: e.memset(ap, val), (), w)

    def SCAN(out, d0, d1, init, op0, op1, r, w):
        s.op("dve", lambda e: e.tensor_tensor_scan(out=out, data0=d0, data1=d1, initial=init, op0=op0, op1=op1), r, w)

    def MM(ps, pairs, r, w):
        n = len(pairs)
        fns = []
        for i, (l, rr) in enumerate(pairs):
            fns.append(lambda e, l=l, rr=rr, i=i: e.matmul(ps, lhsT=l, rhs=rr, start=(i == 0), stop=(i == n - 1)))
        s.pe_group(fns, r, w)

    def TRS(items, r, w):
        fns = [(lambda e, o=o, i=i, d=d: e.transpose(o, i, d)) for (o, i, d) in items]
        s.pe_group(fns, r, w)

    evac_rr = [0]

    def evac_eng():
        evac_rr[0] ^= 1
        return "act" if evac_rr[0] else "dve"

    with ExitStack() as g:
        s.setup(g)

        def T(st, name, shape, dt):
            return st.enter_context(nc.sbuf_tensor("t_" + name, shape, dt))

        pf4 = Ring(g, nc, "pf", 4, [128, 512], F32, psum=True)
        pfs = Ring(g, nc, "pfs", 2, [128, 512], F32, psum=True)
        pf = RingCat(pf4, pfs)
        pb = Ring(g, nc, "pb", 2, [128, 1024], BF16, psum=True)
        identf = T(g, "identf", [128, 128], F32)
        identb = T(g, "identb", [128, 128], BF16)
        cst_t = T(g, "cst_t", [128, 4], F32)
        ones_bf = T(g, "ones_bf", [128, 2], BF16)
        s.dma("sp", identf[:], identf_d, writes=["identf"])
        CP("dve", identb[:], identf[:], ["identf"], ["identb"])
        MEMSET(cst_t[:, 0:1], 1.0, ["cst"]); MEMSET(cst_t[:, 1:2], 0.0, ["cst"]); MEMSET(cst_t[:, 2:3], EPS, ["cst"])
        MEMSET(ones_bf[:], 1.0, ["ones_bf"])
        one_c = cst_t[:, 0:1]
        skc = T(g, "skc", [64, 128], F32); thrc = T(g, "thrc", [64, 128], F32); decb = T(g, "decb", [128, 128], F32)
        hmTs = T(g, "hmTs", [128, 16, 4], BF16)
        wcT = T(g, "wcT", [128, 88, 4], F32)
        pastT = T(g, "pastT", [128, 88, 8], F32)
        u6 = T(g, "u6", [128, 88, 6], F32)
        statr = Ring(g, nc, "stat", 6, [128, 4], F32)

        def rms_col(src, srck, rows, junk):
            sq, sk_ = statr.next()
            ACT(junk[0:rows, :], src[0:rows, :], AF.Square, [srck], [sk_], accum_out=sq[0:rows, 0:1])
            TS(sq[0:rows, 1:2], sq[0:rows, 0:1], 1.0 / D, ALU.mult, [sk_], [sk_], s2=EPS, op1=ALU.add)
            ACT(sq[0:rows, 2:3], sq[0:rows, 1:2], AF.Sqrt, [sk_], [sk_])
            RECIP(sq[0:rows, 3:4], sq[0:rows, 2:3], [sk_], [sk_])
            return sq[0:rows, 3:4], sk_

        def transpose_to(src, srck, rows, ntile, dst, dstk, tok0):
            for kq in range(0, ntile, 4):
                nj = min(4, ntile - kq)
                pt, pk = pb.next()
                TRS([(pt[:, j * 128:j * 128 + rows], src[0:rows, (kq + j) * 128:(kq + j + 1) * 128],
                      identb[0:rows, 0:rows]) for j in range(nj)], [srck, "identb"], [pk])
                CP(evac_eng(), dst[:, kq:kq + nj, tok0:tok0 + rows],
                   pt[:, 0:nj * 128].rearrange("p (j c) -> p j c", c=128)[:, :, 0:rows], [pk], [dstk])

        def rope(x3, cs, rows, nh, tmp, key):
            x1_ = x3[:, :, 0:8]; x2_ = x3[:, :, 8:16]
            cos = cs[0:rows, 0:8].unsqueeze(1).broadcast_to([rows, nh, 8])
            sin = cs[0:rows, 8:16].unsqueeze(1).broadcast_to([rows, nh, 8])
            t = [tmp[0:rows, i, 0:nh, :] for i in range(4)]
            TT(t[0], x1_, cos, ALU.mult, [key, "ropes"], ["ropetmp"])
            TT(t[1], x2_, sin, ALU.mult, [key, "ropes"], ["ropetmp"])
            TT(t[2], x2_, cos, ALU.mult, [key, "ropes"], ["ropetmp"])
            TT(t[3], x1_, sin, ALU.mult, [key, "ropes"], ["ropetmp"])
            TT(x1_, t[0], t[1], ALU.subtract, ["ropetmp"], [key])
            TT(x2_, t[2], t[3], ALU.add, ["ropetmp"], [key])

        def wload(W, K, c0, n, wt, wk, col_off=0):
            KT = K // 128
            for k0 in range(0, KT, 8):
                k1 = min(KT, k0 + 8)
                s.dma("pool", wt[:, k0:k1, col_off:col_off + n],
                      W[k0 * 128:k1 * 128, c0:c0 + n].rearrange("(kt p) n -> p kt n", p=128), writes=[wk])

        def attn_unit(M, qT_ap, kT_ap, NK, vtiles, mask_ap, sink_ap, out_ap, rkeys, outk, R):
            smr, pr, ptr, ar = R
            ps, psk = pf.next()
            MM(ps[0:M, 0:NK], [(qT_ap, kT_ap)], rkeys, [psk])
            sm, smk = smr.next()
            if mask_ap is not None:
                STT(sm[0:M, 0:NK], ps[0:M, 0:NK], 0.125, mask_ap, ALU.mult, ALU.add, [psk, "masks"], [smk])
            else:
                TS(sm[0:M, 0:NK], ps[0:M, 0:NK], 0.125, ALU.mult, [psk], [smk])
            a, ak = ar.next()
            s.op("dve", lambda e: e.reduce_max(out=a[0:M, 0:1], in_=sm[0:M, 0:NK], axis=AX.X), [smk], [ak])
            TS(a[0:M, 1:2], a[0:M, 0:1], sink_ap, ALU.max, [ak, "sinks"], [ak], s2=-1.0, op1=ALU.mult)
            p, pk = pr.next()
            ACT(p[0:M, 0:NK], sm[0:M, 0:NK], AF.Exp, [smk, ak], [pk, ak + "s"], bias=a[0:M, 1:2], accum_out=a[0:M, 2:3])
            ACT(a[0:M, 3:4], sink_ap, AF.Exp, [ak, "sinks"], [ak + "e"], bias=a[0:M, 1:2])
            TT(a[0:M, 4:5], a[0:M, 2:3], a[0:M, 3:4], ALU.add, [ak + "s", ak + "e"], [ak + "d"])
            RECIP(a[0:M, 5:6], a[0:M, 4:5], [ak + "d"], [ak + "d"])
            pt, ptk = pb.next()
            items = []
            off = 0
            for i, (v_ap, nk) in enumerate(vtiles):
                items.append((pt[0:nk, i * 128:i * 128 + M], p[0:M, off:off + nk], identb[0:M, 0:M]))
                off += nk
            TRS(items, [pk, "identb"], [ptk])
            pT, pTk = ptr.next()
            if M == 128 and all(nk == 128 for _, nk in vtiles):
                CP(evac_eng(), pT[:, 0:len(vtiles), :],
                   pt[:, 0:len(vtiles) * 128].rearrange("p (j c) -> p j c", c=128), [ptk], [pTk])
            else:
                for i, (v_ap, nk) in enumerate(vtiles):
                    CP(evac_eng(), pT[0:nk, i, 0:M], pt[0:nk, i * 128:i * 128 + M], [ptk], [pTk])
            po, pok = pf.next()
            MM(po[0:M, 0:64], [(pT[0:nk, i, 0:M], v_ap) for i, (v_ap, nk) in enumerate(vtiles)], [pTk] + rkeys, [pok])
            ACT(out_ap, po[0:M, 0:64], AF.Copy, [pok, ak + "d"], [outk], scale=a[0:M, 5:6])

        FULLCH = [(896, 512), (1408, 512), (1920, 128)]

        def hkeys(t0, nt):
            return ["hT%d" % t for t in range(t0 // 128, (t0 + nt - 1) // 128 + 1)]

        gH = ExitStack()
        hT = T(gH, "hT", [128, 16, 2052], BF16)
        with ExitStack() as gB:
            g_ig = T(gB, "g_ig", [4, 2048], F32); g_fg = T(gB, "g_fg", [4, 2048], F32)
            stA = ExitStack()
            if True:
                st = stA
                xr = Ring(st, nc, "xa", 5, [128, D], F32)
                hnr = Ring(st, nc, "hn", 3, [128, D], BF16)
                junk = T(st, "junkA", [128, D], BF16)
                gbc = T(st, "gpre", [128, D], F32)
                s.dma("sp", gbc[:], g_pre[0].partition_broadcast(128), writes=["gpre"])
                def gen_A():
                    order = [16] + list(range(16))
                    loaded = {}

                    def ldA(t):
                        rows = 128 if t < 16 else 4
                        src = xw[t * 128:(t + 1) * 128, :] if t < 16 else xs
                        xt, xk = xr.next()
                        s.dma("sp", xt[0:rows, :], src, writes=[xk])
                        loaded[t] = (xt, xk)

                    for t in order[0:3]:
                        ldA(t)
                    def a1(idx):
                        t = order[idx]
                        if idx + 3 < len(order):
                            ldA(order[idx + 3])
                        rows = 128 if t < 16 else 4
                        xt, xk = loaded.pop(t)
                        col, ck_ = rms_col(xt, xk, rows, junk)
                        hn, hk = hnr.next()
                        STT(hn[0:rows, :], xt[0:rows, :], col, gbc[0:rows, :], ALU.mult, ALU.mult, [xk, ck_, "gpre"], [hk])
                        return t, rows, hn, hk

                    pend = a1(0)
                    for idx in range(len(order)):
                        cur = pend
                        if idx + 1 < len(order):
                            pend = a1(idx + 1)
                        t, rows, hn, hk = cur
                        transpose_to(hn, hk, rows, 16, hT, "hT%d" % t, t * 128)
                        yield

                gA = gen_A()
                for _ in range(3):
                    next(gA, None)
                if nstages < 2:
                    for _ in gA:
                        pass
            if nstages < 2:
                s.flush()
                stA.close()
                return nc
            with ExitStack() as st:
                wr = Ring(st, nc, "win", 2, [128, 16, 512], BF16)
                stg = Ring(st, nc, "stgB", 4, [128, 512], BF16)
                stgT = Ring(st, nc, "stgT", 2, [128, 4, 128], BF16)
                qfr = Ring(st, nc, "qf", 4, [128, 512], F32)
                zsr = Ring(st, nc, "zsb", 2, [4, 512], F32)
                rtmp = T(st, "ropetmp", [128, 4, 8, 8], F32)
                rcs = T(st, "ropes", [128, 17, 16], F32)
                s.dma("sp", rcs[:, 0:16, :], ropet[0:2048, :].rearrange("(t p) c -> p t c", p=128), writes=["ropes"])
                s.dma("sp", rcs[0:4, 16, :], ropet[2048:2052, :], writes=["ropes"])
                blocks = []
                for i in range(4): blocks.append((ZC["vm"] + 512 * i, 512, "vm", i))
                for i in range(2): blocks.append((ZC["km"] + 512 * i, 512, "km", i))
                blocks.append((ZC["ig"], 8, "gt", 0))
                for i in range(2): blocks.append((ZC["qm"] + 512 * i, 512, "qm", i))
                for i in range(4): blocks.append((ZC["om"] + 512 * i, 512, "om", i))
                for i in range(4): blocks.append((ZC["qa"] + 512 * i, 512, "qa", i))
                blocks.append((ZC["ka"], 512, "kv", 0))

                def form_a(wt, wk, n, t):
                    rows = 128 if t < 16 else 4
                    ps, psk = pf.next()
                    MM(ps[0:rows, 0:n], [(hT[:, kt, t * 128:t * 128 + rows], wt[:, kt, 0:n]) for kt in range(16)],
                       [wk, "hT%d" % t], [psk])
                    return ps, psk

                def form_b(wt, wk, m0, mrows, t0, nt):
                    ps, psk = pf.next()
                    MM(ps[0:mrows, 0:nt], [(wt[:, kt, m0:m0 + mrows], hT[:, kt, t0:t0 + nt]) for kt in range(16)],
                       [wk] + hkeys(t0, nt), [psk])
                    return ps, psk

                def store_tm(ps, psk, dst, func=None):
                    sb, sbk = stg.next()
                    if func is None:
                        CP(evac_eng(), sb[:, :], ps[:, :], [psk], [sbk])
                    else:
                        ACT(sb[:, :], ps[:, :], func, [psk], [sbk])
                    s.dma("sp", dst, sb[:, :], reads=[sbk])

                def jobs(bi, wt, wk):
                    c0, n, kind, i = blocks[bi]
                    ps, psk = form_a(wt, wk, n, 16)
                    zb, zbk = zsr.next()
                    CP("act", zb[0:4, 0:n], ps[0:4, 0:n], [psk], [zbk])
                    s.dma("sp", zs[:, c0:c0 + n], zb[0:4, 0:n], reads=[zbk])
                    if kind == "km":
                        for t in range(16):
                            ps, psk = form_a(wt, wk, n, t)
                            store_tm(ps, psk, zk[t * 128:(t + 1) * 128, i * 512:(i + 1) * 512])
                    if kind in ("qm", "km", "ga", "gb"):
                        dstT = {"qm": zqT, "km": zkT, "ga": sga, "gb": sgb}[kind]
                        for m in range(4):
                            for (t0, nt) in FULLCH:
                                ps, psk = form_b(wt, wk, m * 128, 128, t0, nt)
                                sb, sbk = stg.next()
                                if kind in ("ga", "gb"):
                                    ACT(sb[:, 0:nt], ps[:, 0:nt], AF.Sigmoid, [psk], [sbk])
                                else:
                                    CP(evac_eng(), sb[:, 0:nt], ps[:, 0:nt], [psk], [sbk])
                                s.dma("sp", dstT[(i * 4 + m) * 128:(i * 4 + m + 1) * 128, t0 - 896:t0 - 896 + nt],
                                      sb[:, 0:nt], reads=[sbk])
                    if kind == "vm":
                        for t in range(16):
                            if i == 0:
                                next(gA, None)
                            ps, psk = form_a(wt, wk, n, t)
                            store_tm(ps, psk, zv[t * 128:(t + 1) * 128, i * 512:(i + 1) * 512])
                    if kind == "om":
                        for t in range(7, 16):
                            ps, psk = form_a(wt, wk, n, t)
                            store_tm(ps, psk, zo[(t - 7) * 128:(t - 6) * 128, i * 512:(i + 1) * 512], AF.Sigmoid)
                    if kind == "gt":
                        for gi, gt_ in enumerate((g_ig, g_fg)):
                            for t0 in range(0, 2048, 512):
                                ps, psk = form_b(wt, wk, gi * 4, 4, t0, 512)
                                CP(evac_eng(), gt_[0:4, t0:t0 + 512], ps[0:4, 0:512], [psk], ["g_gate%d" % gi])
                    if kind in ("qa", "kv"):
                        tiles = list(range(7, 16)) if kind == "qa" else list(range(6, 16))

                        def mm(t):
                            ps, psk = form_a(wt, wk, n, t)
                            qf, qfk = qfr.next()
                            CP("act", qf[:, :], ps[:, :], [psk], [qfk])
                            return t, qf, qfk

                        def post(ctx):
                            t, qf, qfk = ctx
                            if kind == "qa":
                                rope(qf[:, :].rearrange("p (h d) -> p h d", d=64), rcs[:, t, :], 128, 8, rtmp, qfk)
                                sb, sbk = stg.next()
                                CP("act", sb[:, :], qf[:, :], [qfk], [sbk])
                                pt, pk = pb.next()
                                TRS([(pt[:, j * 128:(j + 1) * 128], sb[:, j * 128:(j + 1) * 128], identb[:, :])
                                     for j in range(4)], [sbk, "identb"], [pk])
                                sT, sTk = stgT.next()
                                CP(evac_eng(), sT[:, :, :], pt[:, 0:512].rearrange("p (j c) -> p j c", c=128), [pk], [sTk])
                                s.dma("sp", qaT[i * 512:(i + 1) * 512, (t - 7) * 128:(t - 6) * 128]
                                      .rearrange("(j p) c -> p j c", p=128), sT[:, :, :], reads=[sTk])
                            else:
                                rope(qf[:, 0:256].rearrange("p (h d) -> p h d", d=64), rcs[:, t, :], 128, 4, rtmp, qfk)
                                if t == 15:
                                    s.dma("sp", k_p, qf[:, 0:256], reads=[qfk])
                                    s.dma("sp", v_p, qf[:, 256:512], reads=[qfk])
                                sb, sbk = stg.next()
                                CP("act", sb[:, :], qf[:, :], [qfk], [sbk])
                                s.dma("sp", va[(t - 6) * 128:(t - 5) * 128, :], sb[:, 256:512], reads=[sbk])
                                pt, pk = pb.next()
                                TRS([(pt[:, j * 128:(j + 1) * 128], sb[:, j * 128:(j + 1) * 128], identb[:, :])
                                     for j in range(2)], [sbk, "identb"], [pk])
                                sT, sTk = stgT.next()
                                CP(evac_eng(), sT[:, 0:2, :], pt[:, 0:256].rearrange("p (j c) -> p j c", c=128), [pk], [sTk])
                                s.dma("sp", kaT[:, (t - 6) * 128:(t - 5) * 128].rearrange("(j p) c -> p j c", p=128),
                                      sT[:, 0:2, :], reads=[sTk])

                        pend = [mm(tiles[0]), mm(tiles[1])]
                        for idx in range(len(tiles)):
                            if idx + 2 < len(tiles):
                                pend.append(mm(tiles[idx + 2]))
                            post(pend.pop(0))

                nxt = wr.next()
                wload(w_in, D, blocks[0][0], blocks[0][1], nxt[0], nxt[1])
                for bi in range(len(blocks)):
                    cur = nxt
                    if bi + 1 < len(blocks):
                        nxt = wr.next()
                        wload(w_in, D, blocks[bi + 1][0], blocks[bi + 1][1], nxt[0], nxt[1])
                    jobs(bi, cur[0], cur[1])
                s.flush()
            stA.close()
            if nstages < 3:
                return nc
            with ExitStack() as st:
                val = T(st, "val", [4, 2048], F32); ones4 = T(st, "ones4", [4, 2048], F32)
                bif = T(st, "bif", [4, 4], F32)
                e1 = T(st, "e1", [4, 2048], F32); lf = T(st, "lf", [4, 2048], F32); Bc = T(st, "Bc", [4, 2048], F32)
                aa = T(st, "aa", [4, 2048], F32); tmpc = T(st, "tmpc", [4, 2048], F32); Mx = T(st, "Mx", [4, 2048], F32)
                skr = T(st, "skr", [4, 2048], F32); thr_ = T(st, "thr_", [4, 2048], F32)
                Rp = T(st, "Rp", [4, 32], F32); dec = T(st, "dec", [4, 32], F32); mo_ = T(st, "mo_", [4, 2], F32)
                s.dma("sp", val[:], gvalid, writes=["val"])
                s.dma("sp", bif[:, 0:2], b_if[0].rearrange("(two h) -> h two", two=2), writes=["bif"],
                      allow_slow_non_contiguous=True)
                MEMSET(ones4[:], 1.0, ["ones4"])
                TS(bif[:, 2:3], bif[:, 1:2], -1.0, ALU.mult, ["bif"], ["bif2"])
                ACT(e1[:], g_fg[:], AF.Exp, ["bif2"], ["e1"], scale=-1.0, bias=bif[:, 2:3])
                ACT(e1[:], e1[:], AF.Ln, ["e1", "cst"], ["e1"], bias=one_c[0:4, :])
                STT(lf[:], e1[:], -1.0, val[:], ALU.mult, ALU.mult, ["e1", "val"], ["lf"])
                SCAN(Bc[:], ones4[:], lf[:], 0.0, ALU.mult, ALU.add, ["ones4", "lf"], ["Bc"])
                STT(aa[:], g_ig[:], bif[:, 0:1], Bc[:], ALU.add, ALU.subtract, ["bif", "Bc"], ["aa"])
                TS(tmpc[:], val[:], BIG, ALU.mult, ["val"], ["tmpc"], s2=-BIG, op1=ALU.add)
                TT(aa[:], aa[:], val[:], ALU.mult, ["aa", "val"], ["aa"])
                TT(aa[:], aa[:], tmpc[:], ALU.add, ["aa", "tmpc"], ["aa"])
                SCAN(Mx[:], aa[:], aa[:], 0.0, ALU.max, ALU.max, ["aa"], ["Mx"])
                M3 = Mx[:, :].rearrange("p (c l) -> p c l", l=64)
                R = M3[:, :, 63]
                Rb = M3[:, :, 63:64].broadcast_to([4, 32, 64])
                TT(skr[:, :].rearrange("p (c l) -> p c l", l=64), aa[:, :].rearrange("p (c l) -> p c l", l=64), Rb,
                   ALU.subtract, ["aa", "Mx"], ["skr"])
                ACT(skr[:], skr[:], AF.Exp, ["skr"], ["skr"])
                TS(skr[:], skr[:], 0.0625, ALU.mult, ["skr"], ["skr"])
                TT(thr_[:, :].rearrange("p (c l) -> p c l", l=64), Bc[:, :].rearrange("p (c l) -> p c l", l=64), Rb,
                   ALU.add, ["Bc", "Mx"], ["thr_"])
                ACT(thr_[:], thr_[:], AF.Exp, ["thr_"], ["thr_"], scale=-1.0)
                MEMSET(Rp[:, 0:1], 0.0, ["Rp"])
                CP("dve", Rp[:, 1:32], M3[:, 0:31, 63], ["Mx", "Rp"], ["Rp"])
                TT(dec[:], Rp[:], R, ALU.subtract, ["Rp", "Mx"], ["dec"])
                ACT(dec[:], dec[:], AF.Exp, ["dec"], ["dec"])
                TT(mo_[:, 0:1], Bc[:, 2047:2048], Mx[:, 2047:2048], ALU.add, ["Bc", "Mx"], ["mo_"])
                s.dma("sp", m_p.rearrange("o h -> h o"), mo_[:, 0:1], reads=["mo_"])
                for (src, srck, dst, dstk) in ((skr, "skr", skc, "skc"), (thr_, "thr_", thrc, "thrc")):
                    ps, psk = pf.next()
                    TRS([(ps[0:64, c * 4:(c + 1) * 4], src[:, c * 64:(c + 1) * 64], identf[0:4, 0:4]) for c in range(32)],
                        [srck, "identf"], [psk])
                    CP("dve", dst[:, :], ps[0:64, 0:128], [psk], [dstk])
                s.dma("sp", dsc, dec[:], reads=["dec"], writes=["dsc"])
                s.dma("sp", decb[:], dsc.rearrange("h c -> (h c)").partition_broadcast(128), reads=["dsc"], writes=["decb"])
                s.flush()
        if nstages < 4:
            return nc
        with ExitStack() as st:
            S = T(st, "S", [128, 4, 2, 512], F32); nS = T(st, "nS", [128, 4, 2], F32)
            maskT = T(st, "maskT", [64, 64], F32)
            kr = Ring(st, nc, "kch", 2, [64, 1024], BF16); vr = Ring(st, nc, "vch", 2, [64, 2048], BF16)
            kpr = Ring(st, nc, "kp", 2, [64, 1024], BF16)
            qr = Ring(st, nc, "qTc", 2, [128, 8, 64], BF16); ktr = Ring(st, nc, "kTc", 2, [128, 8, 64], BF16)
            sor = Ring(st, nc, "sigo", 2, [64, 2048], BF16); hmr = Ring(st, nc, "hmsb", 2, [64, 2048], BF16)
            cdr = Ring(st, nc, "cdb", 8, [128, 2, 512], BF16); ndr = Ring(st, nc, "ndb", 8, [128, 2], BF16)
            ssr = Ring(st, nc, "stsb", 8, [64, 64], BF16); rsr = Ring(st, nc, "rsd", 8, [64, 4], F32)
            cor = Ring(st, nc, "cout", 2, [128, 4, 256], F32); nout = T(st, "nout", [2, 512], F32)
            s.dma("sp", maskT[:], maskT_d, writes=["maskT"])
            MEMSET(S[:], 0.0, ["S%d" % h for h in range(4)])
            MEMSET(nS[:], 0.0, ["nS%d" % h for h in range(4)])

            def loads(c):
                kc, kk = kr.next(); vc, vk = vr.next()
                s.dma("sp", kc[:], zk[c * 64:(c + 1) * 64, :], writes=[kk])
                s.dma("sp", vc[:], zv[c * 64:(c + 1) * 64, :], writes=[vk])
                res = [kc, kk, vc, vk]
                if c >= 14:
                    fc = c - 14
                    qc, qk = qr.next(); tc_, tk = ktr.next(); so, sok = sor.next()
                    s.dma("sp", qc[:], zqT.rearrange("(a p) n -> p a n", p=128)[:, :, fc * 64:(fc + 1) * 64], writes=[qk])
                    s.dma("sp", tc_[:], zkT.rearrange("(a p) n -> p a n", p=128)[:, :, fc * 64:(fc + 1) * 64], writes=[tk])
                    s.dma("sp", so[:], zo[fc * 64:(fc + 1) * 64, :], writes=[sok])
                    res += [qc, qk, tc_, tk, so, sok]
                return res

            nxt = loads(0)
            for c in range(32):
                cur = nxt
                if c + 1 < 32:
                    nxt = loads(c + 1)
                kc, kk, vc, vk = cur[0:4]
                full = c >= 14
                if full:
                    qc, qk, tc_, tk, so, sok = cur[4:10]
                    hsb, hsk = hmr.next()
                kp, kpk = kpr.next()
                pss, pssk = pfs.next()
                dcols = [decb[:, h * 32 + c:h * 32 + c + 1] for h in range(4)]
                scols = [skc[:, c * 4 + h:c * 4 + h + 1] for h in range(4)]
                for h in range(4):
                    if h % 2 == 0:
                        TS(kp[:, h * 256:(h + 1) * 256], kc[:, h * 256:(h + 1) * 256], scols[h], ALU.mult, [kk, "skc"], [kpk + str(h)])
                    else:
                        ACT(kp[:, h * 256:(h + 1) * 256], kc[:, h * 256:(h + 1) * 256], AF.Copy, [kk, "skc"], [kpk + str(h)],
                            scale=scols[h])
                if full:
                    cds = []
                    for h in range(4):
                        cdb, cdk = cdr.next(); ndb, ndk = ndr.next()
                        ACT(cdb[:, :, :], S[:, h, :, :], AF.Copy, ["S%d" % h, "decb"], [cdk], scale=dcols[h])
                        TS(ndb[:, :], nS[:, h, :], dcols[h], ALU.mult, ["nS%d" % h, "decb"], [ndk])
                        cds.append((cdb, cdk, ndb, ndk))
                    for h in range(4):
                        MM(pss[0:64, h * 64:(h + 1) * 64], [(tc_[:, h * 2 + dt, :], qc[:, h * 2 + dt, :]) for dt in range(2)],
                           [tk, qk], [pssk + "st%d" % h])
                    sss = []
                    for h in range(4):
                        ssb, ssk = ssr.next()
                        STT(ssb[:, :], pss[0:64, h * 64:(h + 1) * 64], scols[h], maskT[:, :], ALU.mult, ALU.mult,
                            [pssk + "st%d" % h, "skc", "maskT"], [ssk])
                        sss.append((ssb, ssk))
                    pns = []
                    for h in range(4):
                        cdb, cdk, ndb, ndk = cds[h]; ssb, ssk = sss[h]
                        pnum, pnk = pf4.next()
                        MM(pnum[0:64, :], [(ssb[:, :], vc[:, h * 512:(h + 1) * 512])] +
                           [(qc[:, h * 2 + dt, :], cdb[:, dt, :]) for dt in range(2)], [ssk, vk, qk, cdk], [pnk])
                        MM(pss[0:64, 256 + h:257 + h], [(ssb[:, :], ones_bf[0:64, 0:1])] +
                           [(qc[:, h * 2 + dt, :], ndb[:, dt:dt + 1]) for dt in range(2)], [ssk, "ones_bf", qk, ndk],
                           [pssk + "dn%d" % h])
                        pns.append((pnum, pnk))
                    for h in range(4):
                        pnum, pnk = pns[h]
                        pdk = pssk + "dn%d" % h
                        pden = pss[0:64, 256 + h:257 + h]
                        rs, rsk = rsr.next()
                        TS(rs[:, 0:1], pden, -1.0, ALU.mult, [pdk], [rsk])
                        STT(rs[:, 2:3], pden, thrc[:, c * 4 + h:c * 4 + h + 1], rs[:, 0:1], ALU.max, ALU.max,
                            [pdk, "thrc", rsk], [rsk])
                        RECIP(rs[:, 1:2], rs[:, 2:3], [rsk], [rsk])
                        STT(hsb[:, h * 512:(h + 1) * 512], pnum[0:64, :], rs[:, 1:2], so[:, h * 512:(h + 1) * 512],
                            ALU.mult, ALU.mult, [pnk, rsk, sok], [hsk + str(h)])
                for hg in range(2):
                    pus = {}
                    for h in (2 * hg, 2 * hg + 1):
                        for dt in range(2):
                            pu, puk = pf4.next()
                            MM(pu[:, :], [(kp[:, h * 256 + dt * 128:h * 256 + (dt + 1) * 128], vc[:, h * 512:(h + 1) * 512])],
                               [kpk + str(h), vk], [puk])
                            pus[(h, dt)] = (pu, puk)
                            MM(pss[:, 264 + h * 2 + dt:265 + h * 2 + dt],
                               [(kp[:, h * 256 + dt * 128:h * 256 + (dt + 1) * 128], ones_bf[0:64, 0:1])],
                               [kpk + str(h), "ones_bf"], [pssk + "un%d_%d" % (h, dt)])
                    for h in (2 * hg, 2 * hg + 1):
                        for dt in range(2):
                            STT(S[:, h, dt, :], S[:, h, dt, :], dcols[h], pus[(h, dt)][0][:, :], ALU.mult, ALU.add,
                                ["S%d" % h, "decb", pus[(h, dt)][1]], ["S%d" % h])
                        STT(nS[:, h, :], nS[:, h, :], dcols[h], pss[:, 264 + h * 2:266 + h * 2], ALU.mult, ALU.add,
                            ["nS%d" % h, "decb", pssk + "un%d_0" % h, pssk + "un%d_1" % h], ["nS%d" % h])
                if full:
                    s.dma("sp", hm[(c - 14) * 64:(c - 13) * 64, :], hsb[:, :], reads=[hsk + str(h) for h in range(4)])
            for h in range(4):
                co, cok = cor.next()
                for half in range(2):
                    ps, psk = pf4.next()
                    TRS([(ps[:, (vv * 2 + dt) * 128:(vv * 2 + dt + 1) * 128],
                          S[:, h, dt, (half * 2 + vv) * 128:(half * 2 + vv + 1) * 128], identf[:, :])
                         for vv in range(2) for dt in range(2)], ["S%d" % h, "identf"], [psk])
                    CP(evac_eng(), co[:, half * 2:half * 2 + 2, :], ps[:, :].rearrange("p (v k) -> p v k", k=256), [psk], [cok])
                s.dma("sp", c_p[h].rearrange("(vt p) k -> p vt k", p=128), co[:, :, :], reads=[cok])
            ps, psk = pf4.next()
            TRS([(ps[0:2, h * 128:(h + 1) * 128], nS[:, h, :], identf[:, :]) for h in range(4)],
                ["nS%d" % h for h in range(4)] + ["identf"], [psk])
            CP("dve", nout[:, :], ps[0:2, :], [psk], ["nout"])
            s.dma("sp", n_p.rearrange("h (dt p) -> dt h p", p=128), nout[:, :].rearrange("d (h p) -> d h p", p=128),
                  reads=["nout"])
            s.flush()
        if nstages < 5:
            return nc
        def gen_B2(st):
            wr2 = Ring(st, nc, "win2", 2, [128, 16, 256], BF16)
            stg2 = Ring(st, nc, "stg2", 4, [128, 512], BF16)
            cols = [(ZC["ga"] + 256 * i, sga, i) for i in range(8)] + [(ZC["gb"] + 256 * i, sgb, i) for i in range(8)]
            CH4 = [(896, 512, 0), (1408, 512, 512), (1920, 128, 1024), (2048, 4, NFULL)]

            def ld(bi):
                wt, wk = wr2.next()
                wload(w_in, D, cols[bi][0], 256, wt, wk)
                return wt, wk

            nxt = ld(0)
            yield
            for bi in range(16):
                wt, wk = nxt
                if bi + 1 < 16:
                    nxt = ld(bi + 1)
                c0, dstT, i = cols[bi]
                for m in range(2):
                    for (t0, nt, o0) in CH4:
                        ps, psk = pf.next()
                        MM(ps[:, 0:nt], [(wt[:, kt, m * 128:(m + 1) * 128], hT[:, kt, t0:t0 + nt]) for kt in range(16)], [wk], [psk])
                        sb, sbk = stg2.next()
                        CP(evac_eng(), sb[:, 0:nt], ps[:, 0:nt], [psk], [sbk])
                        s.dma("sp", dstT[(i * 2 + m) * 128:(i * 2 + m + 1) * 128, o0:o0 + nt], sb[:, 0:nt], reads=[sbk])
                        yield

        def gen_S(st):
            zsm = T(st, "zsm", [4, 8], F32); stm = T(st, "stm", [4, 4], F32); bib = T(st, "bib", [4, 8], F32)
            sc = T(st, "sc", [4, 24], F32)
            scal = T(st, "scal", [4, 12], F32)
            scb = T(st, "scb", [128, 48], F32)
            zsv = T(st, "zsv", [4, 2048], F32)
            vT = T(st, "vT", [128, 32, 4], F32)
            s.dma("sp", zsm[:], zs[:, ZC["ig"]:ZC["ig"] + 8], writes=["zsm"])
            s.dma("sp", stm[:], st_m, writes=["stm"])
            s.dma("sp", bib[:], b_if[0].partition_broadcast(4), writes=["bib"])
            s.dma("sp", zsv[:], zs[:, ZC["vm"]:ZC["vm"] + 2048], writes=["zsv"])
            qs = T(st, "qs", [4, 2048], F32); ks = T(st, "ks", [4, 512], F32)
            rcs = T(st, "ropesS", [4, 16], F32); sinkc = T(st, "sinkc", [8, 4], F32)
            s.dma("sp", qs[:], zs[:, ZC["qa"]:ZC["qa"] + 2048], writes=["qs"])
            s.dma("sp", ks[:], zs[:, ZC["ka"]:ZC["ka"] + 512], writes=["ks"])
            s.dma("sp", rcs[:], ropet[2048:2052, :], writes=["ropes"])
            s.dma("sp", sinkc[:], sinks[0].rearrange("(g j) -> j g", j=8), writes=["sinksS"], allow_slow_non_contiguous=True)
            yield
            TT(sc[:, 0:8], zsm[:, :], bib[:, :], ALU.add, ["zsm", "bib"], ["sc0"])
            ACT(sc[:, 8:12], sc[:, 4:8], AF.Exp, ["sc0"], ["sc1"], scale=-1.0)
            ACT(sc[:, 8:12], sc[:, 8:12], AF.Ln, ["sc1", "cst"], ["sc1"], bias=one_c[0:4, :])
            STT(sc[:, 12:16], sc[:, 8:12], -1.0, stm[:, :], ALU.mult, ALU.add, ["sc1", "stm"], ["sc2"])
            TT(sc[:, 16:20], sc[:, 0:4], sc[:, 12:16], ALU.max, ["sc0", "sc2"], ["sc3"])
            s.dma("sp", m_s, sc[:, 16:20], reads=["sc3"])
            TT(scal[:, 0:4], sc[:, 12:16], sc[:, 16:20], ALU.subtract, ["sc2", "sc3"], ["scal0"])
            TT(scal[:, 4:8], sc[:, 0:4], sc[:, 16:20], ALU.subtract, ["sc0", "sc3"], ["scal1"])
            ACT(scal[:, 0:8], scal[:, 0:8], AF.Exp, ["scal0", "scal1"], ["scal01"])
            TS(scal[:, 4:8], scal[:, 4:8], 0.0625, ALU.mult, ["scal01"], ["scal01"])
            ACT(scal[:, 8:12], sc[:, 16:20], AF.Exp, ["sc3"], ["scal2"], scale=-1.0)
            s.dma("sp", sscr, scal[:, :], reads=["scal01", "scal2"], writes=["sscr"])
            yield
            s.dma("sp", scb[:], sscr.rearrange("j q -> (j q)").partition_broadcast(128), reads=["sscr"], writes=["scb"])
            for half in range(2):
                if half == 1:
                    s.dma("sp", zsv[:], zs[:, ZC["om"]:ZC["om"] + 2048], reads=["zsv"], writes=["zsv"])
                    ACT(zsv[:, :], zsv[:, :], AF.Sigmoid, ["zsv"], ["zsv"])
                ps, psk = pf.next()
                TRS([(ps[:, j * 4:(j + 1) * 4], zsv[:, j * 128:(j + 1) * 128], identf[0:4, 0:4])
                     for j in range(16)], ["zsv", "identf"], [psk])
                CP("dve", vT[:, half * 16:(half + 1) * 16, :], ps[:, 0:64].rearrange("p (j c) -> p j c", c=4), [psk], ["vT"])
            csr = Ring(st, nc, "Cs", 2, [128, 4, 256], F32); cnr = Ring(st, nc, "Cn", 2, [128, 4, 256], F32)
            qbr_ = Ring(st, nc, "qbc", 2, [128, 256], F32); kbr_ = Ring(st, nc, "kbc", 2, [128, 256], F32)
            nbr_ = Ring(st, nc, "nbc", 2, [128, 256], F32)
            smr_ = Ring(st, nc, "ssm", 3, [128, 16], F32)
            junkS = T(st, "junkS", [128, 256], F32)
            def loadsS(u):
                j, h = u // 4, u % 4
                Cs, Csk = csr.next(); qb_, qbk = qbr_.next(); kb_, kbk = kbr_.next(); nb_, nbk = nbr_.next()
                s.dma("sp", Cs[:], st_c[j, h].rearrange("(vt p) k -> p vt k", p=128),
                      writes=[Csk] + [Csk + "v%d" % vt for vt in range(4)])
                s.dma("sp", qb_[:], zs[j, ZC["qm"] + h * 256:ZC["qm"] + (h + 1) * 256].partition_broadcast(128), writes=[qbk])
                s.dma("sp", kb_[:], zs[j, ZC["km"] + h * 256:ZC["km"] + (h + 1) * 256].partition_broadcast(128), writes=[kbk])
                s.dma("sp", nb_[:], st_n[j, h].partition_broadcast(128), writes=[nbk])
                return Cs, Csk, qb_, qbk, kb_, kbk, nb_, nbk

            nxtS = loadsS(0)
            yield
            for u in range(16):
                j, h = u // 4, u % 4
                Cs, Csk, qb_, qbk, kb_, kbk, nb_, nbk = nxtS
                if u + 1 < 16:
                    nxtS = loadsS(u + 1)
                if True:
                    Cn, Cnk = cnr.next()
                    sm_, smk = smr_.next()
                    dcol = scb[:, j * 12 + h:j * 12 + h + 1]
                    kcol = scb[:, j * 12 + 4 + h:j * 12 + 4 + h + 1]
                    tcol = scb[:, j * 12 + 8 + h:j * 12 + 8 + h + 1]
                    TS(sm_[:, 0:4], vT[:, h * 4:(h + 1) * 4, j], kcol, ALU.mult, ["vT", "scb"], [smk])
                    for vt in range(4):
                        ACT(Cs[:, vt, :], Cs[:, vt, :], AF.Copy, [Csk, "scb"], [Csk + "v%d" % vt], scale=dcol)
                    for vt in range(4):
                        STT(Cn[:, vt, :], kb_[:, :], sm_[:, vt:vt + 1], Cs[:, vt, :], ALU.mult, ALU.add,
                            [kbk, smk, Csk + "v%d" % vt], [Cnk + "v%d" % vt])
                        STT(junkS[:, :], Cn[:, vt, :], 1.0, qb_[:, :], ALU.mult, ALU.mult, [Cnk + "v%d" % vt, qbk], [smk + "n%d" % vt],
                            accum=sm_[:, 4 + vt:5 + vt])
                    s.dma("sp", c_s[j, h].rearrange("(vt p) k -> p vt k", p=128), Cn[:, :, :],
                          reads=[Cnk + "v%d" % vt for vt in range(4)])
                    TS(nb_[:, :], nb_[:, :], dcol, ALU.mult, [nbk, "scb"], [nbk])
                    STT(nb_[:, :], kb_[:, :], kcol, nb_[:, :], ALU.mult, ALU.add, [kbk, nbk, "scb"], [nbk])
                    s.dma("sp", n_s[j, h:h + 1, :], nb_[0:1, :], reads=[nbk])
                    STT(junkS[:, :], nb_[:, :], 1.0, qb_[:, :], ALU.mult, ALU.mult, [nbk, qbk], [smk + "d"], accum=sm_[:, 8:9])
                    TS(sm_[:, 9:10], sm_[:, 8:9], -1.0, ALU.mult, [smk + "d"], [smk + "d"])
                    STT(sm_[:, 11:12], sm_[:, 8:9], tcol, sm_[:, 9:10], ALU.max, ALU.max, [smk + "d", "scb"], [smk + "d"])
                    RECIP(sm_[:, 10:11], sm_[:, 11:12], [smk + "d"], [smk + "d"])
                    STT(hmTs[:, h * 4:(h + 1) * 4, j], sm_[:, 4:8], sm_[:, 10:11], vT[:, 16 + h * 4:16 + (h + 1) * 4, j],
                        ALU.mult, ALU.mult, [smk + "n%d" % vt for vt in range(4)] + [smk + "d", "vT"], ["hmTs"])
                yield
            qsb = T(st, "qsb", [4, 2048], BF16); ksb = T(st, "ksb", [4, 512], BF16)
            rtmp = T(st, "ropetmpS", [4, 4, 32, 8], F32)
            qTs = T(st, "qTs", [64, 32, 4], BF16); knT = T(st, "knT", [64, 4, 4], BF16)
            vnew = T(st, "vnew", [1, 4, 256], BF16)
            hos = T(st, "hos", [8, 4, 4, 64], BF16)
            rope(qs[:, :].rearrange("p (h d) -> p h d", d=64), rcs, 4, 32, rtmp, "qs")
            rope(ks[:, 0:256].rearrange("p (h d) -> p h d", d=64), rcs, 4, 4, rtmp, "ks")
            s.dma("sp", k_s[:, 127, :], ks[:, 0:256], reads=["ks"])
            s.dma("sp", v_s[:, 127, :], ks[:, 256:512], reads=["ks"])
            s.dma("sp", k_s[:, 0:127, :], ck[:, 1:128, :])
            s.dma("sp", v_s[:, 0:127, :], cv[:, 1:128, :])
            CP("act", qsb[:], qs[:], ["qs"], ["qsb"])
            CP("act", ksb[:], ks[:], ["ks"], ["ksb"])
            s.dma("sp", vscr, ksb[:, 256:512], reads=["ksb"], writes=["vscr"])
            yield
            s.dma("sp", vnew[:], vscr.rearrange("(o j) c -> o j c", o=1), reads=["vscr"], writes=["vnew"])
            pt, pk = pb.next()
            TRS([(pt[0:64, hd * 4:(hd + 1) * 4], qsb[:, hd * 64:(hd + 1) * 64], identb[0:4, 0:4]) for hd in range(32)],
                ["qsb", "identb"], [pk])
            CP("dve", qTs[:, :, :], pt[0:64, 0:128].rearrange("p (h c) -> p h c", c=4), [pk], ["qTs"])
            pt, pk = pb.next()
            TRS([(pt[0:64, gq * 4:(gq + 1) * 4], ksb[:, gq * 64:(gq + 1) * 64], identb[0:4, 0:4]) for gq in range(4)],
                ["ksb", "identb"], [pk])
            CP("dve", knT[:, :, :], pt[0:64, 0:16].rearrange("p (h c) -> p h c", c=4), [pk], ["knT"])
            kcr = Ring(st, nc, "kcb", 2, [128, 256], BF16); vcr = Ring(st, nc, "vcb", 2, [128, 256], BF16)
            kter = Ring(st, nc, "kTe", 2, [64, 132], BF16)
            R = (Ring(st, nc, "smS", 3, [128, 256], F32), Ring(st, nc, "ppS", 3, [128, 256], BF16),
                 Ring(st, nc, "pTS", 3, [128, 2, 128], BF16), Ring(st, nc, "astS", 4, [128, 8], F32))
            def loadsC(j):
                kcb, kck = kcr.next(); vcb, vck = vcr.next()
                s.dma("pool", kcb[:], ck[j], writes=[kck])
                s.dma("pool", vcb[:], cv[j], writes=[vck])
                return kcb, kck, vcb, vck

            nxtC = loadsC(0)
            yield
            for j in range(4):
                kcb, kck, vcb, vck = nxtC
                if j + 1 < 4:
                    nxtC = loadsC(j + 1)
                for gq in range(4):
                    pt, pk = pb.next()
                    TRS([(pt[0:64, 0:128], kcb[:, gq * 64:(gq + 1) * 64], identb[:, :])], [kck, "identb"], [pk])
                    kTe, kTk = kter.next()
                    CP(evac_eng(), kTe[:, 0:128], pt[0:64, 0:128], [pk], [kTk])
                    CP("dve", kTe[:, 128:129], knT[:, gq, j:j + 1], ["knT", kTk], [kTk])
                    attn_unit(8, qTs[:, gq * 8:(gq + 1) * 8, j], kTe[:, 0:129], 129,
                              [(vcb[:, gq * 64:(gq + 1) * 64], 128), (vnew[0:1, j, gq * 64:(gq + 1) * 64], 1)],
                              None, sinkc[:, gq:gq + 1], hos[:, j, gq, :], ["qTs", kTk, vck, "vnew", "sinksS"], "hos", R)
                    yield
            s.dma("sp", ha[NFULL:NFS, :].rearrange("j (g hh d) -> hh j g d", g=4, hh=8), hos[:, :, :, :], reads=["hos"])
        with ExitStack() as st:
            masks_sb = T(st, "masks", [128, 2, 256], F32)
            maskb = T(st, "maskb", [128, 2, 512], BF16)
            sink_bc = T(st, "sinks", [128, 32], F32); nsink = T(st, "nsinks", [128, 32], F32)
            s.dma("sp", masks_sb[:], masks.rearrange("m p k -> p m k"), writes=["masks"])
            s.dma("sp", sink_bc[:], sinks[0].partition_broadcast(128), writes=["sinks"])
            TS(nsink[:], sink_bc[:], -1.0, ALU.mult, ["sinks"], ["nsinks"])
            for mi_ in range(2):
                for rep in range(2):
                    CP("dve", maskb[:, mi_, rep * 256:(rep + 1) * 256], masks_sb[:, mi_, :], ["masks"], ["maskb"])
            qbr = Ring(st, nc, "qblk", 2, [64, 32, 128], BF16)
            kbr = Ring(st, nc, "kblk", 2, [64, 4, 256], BF16); vbr = Ring(st, nc, "vblk", 2, [128, 2, 256], BF16)
            har = Ring(st, nc, "hasb", 2, [128, 2048], BF16)
            ar = Ring(st, nc, "ast", 8, [128, 16], F32); pr = Ring(st, nc, "pp", 6, [128, 512], BF16)
            ptr = Ring(st, nc, "pT", 4, [128, 4, 128], BF16)

            def loadsE(f):
                qb, qbk = qbr.next(); kb, kbk = kbr.next(); vb, vbk = vbr.next()
                s.dma("sp", qb[:], qaT.rearrange("(h d) t -> d h t", d=64)[:, :, f * 128:(f + 1) * 128], writes=[qbk])
                s.dma("sp", kb[:], kaT.rearrange("(h d) t -> d h t", d=64)[:, :, f * 128:f * 128 + 256], writes=[kbk])
                s.dma("sp", vb[:], va[f * 128:f * 128 + 256, :].rearrange("(kt p) c -> p kt c", p=128), writes=[vbk])
                return qb, qbk, kb, kbk, vb, vbk

            Bk = {0: loadsE(0), 1: loadsE(1)}
            Hs = {}

            def part1(f, hp):
                qb, qbk, kb, kbk, vb, vbk = Bk[f]
                gq = hp // 4
                mi = 1 if f == 1 else 0
                ps, psk = pf.next()
                s.pe_group([
                    lambda e: e.matmul(ps[:, 0:512], lhsT=identb[:, :], rhs=maskb[:, mi, :], start=True, stop=False),
                    lambda e: e.matmul(ps[:, 0:256], lhsT=qb[:, 2 * hp, :], rhs=kb[:, gq, :], start=False, stop=False),
                    lambda e: e.matmul(ps[:, 256:512], lhsT=qb[:, 2 * hp + 1, :], rhs=kb[:, gq, :], start=False, stop=True),
                ], [qbk, kbk, "maskb", "identb"], [psk])
                a, ak = ar.next()
                s.op("dve", lambda e: e.reduce_max(out=a[:, 0:2], in_=ps[:, 0:512].rearrange("p (h k) -> p h k", k=256),
                                                   axis=AX.X), [psk], [ak])
                TS(a[:, 2:4], a[:, 0:2], -0.125, ALU.mult, [ak], [ak])
                TT(a[:, 4:6], a[:, 2:4], nsink[:, 2 * hp:2 * hp + 2], ALU.min, [ak, "nsinks"], [ak])
                TT(a[:, 8:10], sink_bc[:, 2 * hp:2 * hp + 2], a[:, 4:6], ALU.add, [ak, "sinks"], [ak + "x"])
                p, pk = pr.next()
                ACT(p[:, 0:256], ps[:, 0:256], AF.Exp, [psk, ak], [pk, ak + "s0"], scale=0.125, bias=a[:, 4:5],
                    accum_out=a[:, 6:7])
                ACT(p[:, 256:512], ps[:, 256:512], AF.Exp, [psk, ak], [pk, ak + "s1"], scale=0.125, bias=a[:, 5:6],
                    accum_out=a[:, 7:8])
                ACT(a[:, 10:12], a[:, 8:10], AF.Exp, [ak + "x"], [ak + "e"])
                return (f, hp, gq, p, pk, a, ak)

            def part2a(ctx):
                f, hp, gq, p, pk, a, ak = ctx
                pt, ptk = pb.next()
                TRS([(pt[:, (hh * 2 + kt) * 128:(hh * 2 + kt + 1) * 128], p[:, hh * 256 + kt * 128:hh * 256 + (kt + 1) * 128],
                      identb[:, :]) for hh in range(2) for kt in range(2)], [pk, "identb"], [ptk])
                pT, pTk = ptr.next()
                CP("act", pT[:, :, :], pt[:, 0:512].rearrange("p (j c) -> p j c", c=128), [ptk], [pTk])
                return ctx + (pT, pTk)

            def part2b(ctx):
                f, hp, gq, p, pk, a, ak, pT, pTk = ctx
                vb, vbk = Bk[f][4:6]
                if f not in Hs:
                    Hs[f] = har.next()
                hsb, hsk = Hs[f]
                TT(a[:, 12:14], a[:, 6:8], a[:, 10:12], ALU.add, [ak + "s0", ak + "s1", ak + "e"], [ak + "d"])
                RECIP(a[:, 14:16], a[:, 12:14], [ak + "d"], [ak + "d"])
                po, pok = pf.next()
                for hh in range(2):
                    MM(po[:, hh * 64:(hh + 1) * 64], [(pT[:, hh * 2 + kt, :], vb[:, kt, gq * 64:(gq + 1) * 64]) for kt in range(2)],
                       [pTk, vbk], [pok])
                TT(hsb[:, hp * 128:(hp + 1) * 128].rearrange("p (h d) -> p h d", d=64),
                   po[:, 0:128].rearrange("p (h d) -> p h d", d=64),
                   a[:, 14:16].unsqueeze(2).broadcast_to([128, 2, 64]), ALU.mult, [pok, ak + "d"], [hsk + "_%d" % hp])
                if hp == 15:
                    s.dma("sp", ha[f * 128:(f + 1) * 128, :], hsb[:, :], reads=[hsk + "_%d" % q for q in range(16)])
                    if f + 2 < 9:
                        Bk[f + 2] = loadsE(f + 2)

            units = [(f, hp) for f in range(9) for hp in range(16)]
            gS = gen_S(st)
            gB2 = gen_B2(st)
            next(gB2)
            SKEW = 3
            n_u = len(units)
            pend = [part1(*units[i]) for i in range(SKEW)]
            mid = None
            for i in range(n_u + 1):
                if mid is not None:
                    part2b(mid)
                    mid = None
                if i < n_u:
                    mid = part2a(pend.pop(0))
                    if i + SKEW < n_u:
                        pend.append(part1(*units[i + SKEW]))
                if i % 3 == 2:
                    next(gS, None)
                next(gB2, None)
            for _ in gS:
                pass
            for _ in gB2:
                pass
            s.flush()
        gH.close()
        if nstages < 6:
            return nc
        if nstages < 7:
            return nc
        with ExitStack() as gX:
            xn2T = None
            mixT = T(gX, "mixT", [128, 16, NFS], BF16)
            with ExitStack() as st:
                hmT = T(st, "hmT", [128, 16, NFS], BF16); haT = T(st, "haT", [128, 16, NFS], BF16)
                ldc = Ring(st, nc, "ldc", 4, [8, 1408], F32)

                def prep_load(pi):
                    lt, lk = ldc.next()
                    s.dma("sp", lt[0:3, :], w_cv[:, pi * 1408:(pi + 1) * 1408], writes=[lk])
                    s.dma("sp", lt[3:4, :], b_cv[:, pi * 1408:(pi + 1) * 1408], writes=[lk])
                    lt2, lk2 = ldc.next()
                    s.dma("sp", lt2[0:8, :], cst[:, pi * 1408:(pi + 1) * 1408], writes=[lk2])
                    return lt, lk, lt2, lk2

                def prep_piece(pi, ld_):
                    lt, lk, lt2, lk2 = ld_
                    ps, psk = pf.next()
                    TRS([(ps[:, j * 4:(j + 1) * 4], lt[0:4, j * 128:(j + 1) * 128], identf[0:4, 0:4]) for j in range(11)],
                        [lk, "identf"], [psk])
                    CP("dve", wcT[:, pi * 11:(pi + 1) * 11, :], ps[:, 0:44].rearrange("p (j c) -> p j c", c=4), [psk], ["wcT"])
                    ps, psk = pf.next()
                    TRS([(ps[:, j * 8:(j + 1) * 8], lt2[0:8, j * 128:(j + 1) * 128], identf[0:8, 0:8]) for j in range(11)],
                        [lk2, "identf"], [psk])
                    CP("dve", pastT[:, pi * 11:(pi + 1) * 11, :], ps[:, 0:88].rearrange("p (j c) -> p j c", c=8), [psk], ["pastT"])

                s.dma("sp", conv_s[:, 0, :], cst.rearrange("(j r) c -> j r c", r=2)[:, 1, :])
                prep_ld = {0: prep_load(0)}
                prep_i = [0]

                def prep_step():
                    pi = prep_i[0]
                    if pi >= 8:
                        return
                    if pi + 1 < 8:
                        prep_ld[pi + 1] = prep_load(pi + 1)
                    prep_piece(pi, prep_ld.pop(pi))
                    prep_i[0] += 1

                ld = Ring(st, nc, "ldF", 2, [128, 2048], BF16)
                for (src, dstT, nm) in ((hm, hmT, "hmT"), (ha, haT, "haT")):
                    for f in range(10):
                        rows = 128 if f < 9 else 4
                        if nm == "hmT" and f == 9:
                            continue
                        lt, lk = ld.next()
                        s.dma("sp", lt[0:rows, :], src[f * 128:f * 128 + rows, :], writes=[lk])
                        transpose_to(lt, lk, rows, 16, dstT, nm, f * 128)
                        prep_step()
                CP("dve", hmT[:, :, NFULL:NFS], hmTs[:, :, :], ["hmTs", "hmT"], ["hmT"])
                wra = Ring(st, nc, "wa", 2, [128, 16, 256], BF16); wrb = Ring(st, nc, "wb", 2, [128, 16, 256], BF16)
                gar = Ring(st, nc, "gat", 3, [128, 512], BF16); gbr = Ring(st, nc, "gbt", 3, [128, 512], BF16)
                t1r = Ring(st, nc, "t1", 2, [128, 512], F32); t2r = Ring(st, nc, "t2", 2, [128, 512], F32)
                CH = [(0, 512), (512, 512), (1024, NFS - 1024)]

                def ldw(bi):
                    a_, ak_ = wra.next(); b_, bk_ = wrb.next()
                    wload(w_a, D, bi * 256, 256, a_, ak_)
                    wload(w_b, D, bi * 256, 256, b_, bk_)
                    return a_, ak_, b_, bk_

                nxt = ldw(0)
                for bi in range(8):
                    wa_, wak, wb_, wbk = nxt
                    if bi + 1 < 8:
                        nxt = ldw(bi + 1)
                    for m in range(2):
                        mt = bi * 2 + m
                        for (t0, nt) in CH:
                            ga_, gak = gar.next(); gb_, gbk = gbr.next()
                            s.dma("sp", ga_[:, 0:nt], sga[mt * 128:(mt + 1) * 128, t0:t0 + nt], writes=[gak])
                            s.dma("sp", gb_[:, 0:nt], sgb[mt * 128:(mt + 1) * 128, t0:t0 + nt], writes=[gbk])
                            ACT(ga_[:, 0:nt], ga_[:, 0:nt], AF.Sigmoid, [gak], [gak])
                            ACT(gb_[:, 0:nt], gb_[:, 0:nt], AF.Sigmoid, [gbk], [gbk])
                            pa, pak = pf.next()
                            MM(pa[:, 0:nt], [(wa_[:, kt, m * 128:(m + 1) * 128], hmT[:, kt, t0:t0 + nt]) for kt in range(16)],
                               [wak, "hmT"], [pak])
                            pb2, pbk = pf.next()
                            MM(pb2[:, 0:nt], [(wb_[:, kt, m * 128:(m + 1) * 128], haT[:, kt, t0:t0 + nt]) for kt in range(16)],
                               [wbk, "haT"], [pbk])
                            t1, t1k = t1r.next(); t2, t2k = t2r.next()
                            TT(t1[:, 0:nt], pa[:, 0:nt], ga_[:, 0:nt], ALU.mult, [pak, gak], [t1k])
                            TT(t2[:, 0:nt], pb2[:, 0:nt], gb_[:, 0:nt], ALU.mult, [pbk, gbk], [t2k])
                            TT(mixT[:, mt, t0:t0 + nt], t1[:, 0:nt], t2[:, 0:nt], ALU.add, [t1k, t2k], ["mixT"])
                s.flush()
            if nstages < 8:
                return nc
            xn2T = T(gX, "xn2T", [128, 16, NFS], BF16)
            with ExitStack() as st:
                wo_all = T(st, "wo_all", [128, 16, D], BF16)
                for bi in range(4):
                    wload(w_o, D, bi * 512, 512, wo_all, "wo%d" % bi, col_off=bi * 512)
                gpm = T(st, "gpm", [128, D], F32); gpf = T(st, "gpf", [128, D], F32)
                s.dma("sp", gpm[:], g_pm[0].partition_broadcast(128), writes=["gpm"])
                s.dma("sp", gpf[:], g_pf[0].partition_broadcast(128), writes=["gpf"])
                xr = Ring(st, nc, "xt", 2, [128, D], F32); mor = Ring(st, nc, "mot", 2, [128, D], F32)
                x1r = Ring(st, nc, "x1t", 1, [128, D], F32); xnr = Ring(st, nc, "xnt", 1, [128, D], BF16)

                def ldx(f):
                    rows = 128 if f < 9 else 4
                    xt, xk = xr.next()
                    s.dma("sp", xt[0:rows, :], xw[(f + 7) * 128:(f + 8) * 128, :] if f < 9 else xs, writes=[xk])
                    return xt, xk

                def mmF(f):
                    rows = 128 if f < 9 else 4
                    mt_, mk = mor.next()
                    pss_ = []
                    for bi in range(4):
                        ps, psk = pf.next()
                        MM(ps[0:rows, :], [(mixT[:, kt, f * 128:f * 128 + rows], wo_all[:, kt, bi * 512:(bi + 1) * 512])
                                           for kt in range(16)], ["wo%d" % bi, "mixT"], [psk])
                        pss_.append((ps, psk))
                    return mt_, mk, pss_, rows

                def evF(ctx):
                    mt_, mk, pss_, rows = ctx
                    for bi, (ps, psk) in enumerate(pss_):
                        CP(evac_eng(), mt_[0:rows, bi * 512:(bi + 1) * 512], ps[0:rows, :], [psk], [mk + "_%d" % bi])

                nxt = ldx(0)
                nmm = mmF(0)
                evF(nmm)
                for f in range(10):
                    rows = 128 if f < 9 else 4
                    xt, xk = nxt
                    mt_, mk = nmm[0:2]
                    if f + 1 < 10:
                        nxt = ldx(f + 1)
                        nmm = mmF(f + 1)
                    mks = [mk + "_%d" % bi for bi in range(4)]
                    sq, sk_ = statr.next()
                    x1t, x1k = x1r.next()
                    xn, xnk = xnr.next()
                    ACT(x1t[0:rows, :], mt_[0:rows, :], AF.Square, mks, [sk_, x1k], accum_out=sq[0:rows, 0:1])
                    TS(sq[0:rows, 1:2], sq[0:rows, 0:1], 1.0 / D, ALU.mult, [sk_], [sk_], s2=EPS, op1=ALU.add)
                    ACT(sq[0:rows, 2:3], sq[0:rows, 1:2], AF.Sqrt, [sk_], [sk_])
                    RECIP(sq[0:rows, 3:4], sq[0:rows, 2:3], [sk_], [sk_])
                    STT(mt_[0:rows, :], mt_[0:rows, :], sq[0:rows, 3:4], gpm[0:rows, :], ALU.mult, ALU.mult,
                        mks + [sk_, "gpm"], mks)
                    TT(x1t[0:rows, :], mt_[0:rows, :], xt[0:rows, :], ALU.add, mks + [xk, x1k], [x1k])
                    s.dma("sp", x1[f * 128:f * 128 + rows, :], x1t[0:rows, :], reads=[x1k])
                    sq2, sk2 = statr.next()
                    ACT(xn[0:rows, :], x1t[0:rows, :], AF.Square, [x1k], [sk2, xnk], accum_out=sq2[0:rows, 0:1])
                    TS(sq2[0:rows, 1:2], sq2[0:rows, 0:1], 1.0 / D, ALU.mult, [sk2], [sk2], s2=EPS, op1=ALU.add)
                    ACT(sq2[0:rows, 2:3], sq2[0:rows, 1:2], AF.Sqrt, [sk2], [sk2])
                    RECIP(sq2[0:rows, 3:4], sq2[0:rows, 2:3], [sk2], [sk2])
                    STT(xn[0:rows, :], x1t[0:rows, :], sq2[0:rows, 3:4], gpf[0:rows, :], ALU.mult, ALU.mult,
                        [x1k, sk2, "gpf", xnk], [xnk])
                    if f + 1 < 10:
                        evF(nmm)
                    transpose_to(xn, xnk, rows, 16, xn2T, "xn2T", f * 128)
                s.flush()
            if nstages < 9:
                return nc
            with ExitStack() as st:
                wur = Ring(st, nc, "wu", 2, [128, 16, 1024], BF16)
                ur = Ring(st, nc, "usb", 4, [128, 1030], F32)
                cr = Ring(st, nc, "csb", 4, [128, 1028], F32)
                yr = Ring(st, nc, "ysb", 2, [128, 1028], BF16)
                UCH = [(126, 512, 0), (638, 512, 512), (1150, 6, 1024)]

                def ldu(q):
                    wt, wk = wur.next()
                    wload(w_up, D, q * 512, 512, wt, wk, col_off=0)
                    wload(w_up, D, DFF + q * 512, 512, wt, wk, col_off=512)
                    return wt, wk

                nxt = ldu(0)
                for i in range(44):
                    if i % 4 == 0:
                        wt, wk = nxt
                        if i // 4 + 1 < 11:
                            nxt = ldu(i // 4 + 1)
                    r4 = i % 4
                    cs_ = []
                    for gi in range(2):
                        ct = gi * 44 + i
                        u, uk = ur.next()
                        for (t0, nt, o0) in UCH:
                            ps, psk = pf.next()
                            MM(ps[:, 0:nt], [(wt[:, kt, gi * 512 + r4 * 128:gi * 512 + (r4 + 1) * 128], xn2T[:, kt, t0:t0 + nt])
                                             for kt in range(16)], [wk, "xn2T"], [psk])
                            CP("act", u[:, o0:o0 + nt], ps[:, 0:nt], [psk], [uk])
                        c_, ck_ = cr.next()
                        ACT(c_[:, 0:1024], u[:, 2:1026], AF.Identity, [uk, "wcT"], [ck_], scale=wcT[:, ct, 2:3], bias=wcT[:, ct, 3:4])
                        STT(c_[:, 0:1024], u[:, 1:1025], wcT[:, ct, 1:2], c_[:, 0:1024], ALU.mult, ALU.add, [uk, "wcT", ck_], [ck_])
                        STT(c_[:, 0:1024], u[:, 0:1024], wcT[:, ct, 0:1], c_[:, 0:1024], ALU.mult, ALU.add, [uk, "wcT", ck_], [ck_])
                        ACT(c_[:, 1024:1028], u[:, 1026:1030], AF.Identity, [uk, "wcT"], [ck_ + "s"], scale=wcT[:, ct, 2:3],
                            bias=wcT[:, ct, 3:4])
                        p3 = pastT[:, ct, :].rearrange("p (j r) -> p j r", r=2)
                        STT(c_[:, 1024:1028], p3[:, :, 1], wcT[:, ct, 1:2], c_[:, 1024:1028], ALU.mult, ALU.add,
                            ["pastT", "wcT", ck_ + "s"], [ck_ + "s"])
                        STT(c_[:, 1024:1028], p3[:, :, 0], wcT[:, ct, 0:1], c_[:, 1024:1028], ALU.mult, ALU.add,
                            ["pastT", "wcT", ck_ + "s"], [ck_ + "s"])
                        CP("dve", u6[:, ct, :], u[:, 1024:1030], [uk], ["u6"])
                        cs_.append((c_, ck_))
                    (cg, cgk), (cv_, cvk) = cs_
                    ACT(cg[:, :], cg[:, :], AF.Gelu_apprx_tanh, [cgk, cgk + "s"], [cgk, cgk + "s"])
                    ysb, ysk = yr.next()
                    TT(ysb[:, :], cg[:, :], cv_[:, :], ALU.mult, [cgk, cgk + "s", cvk, cvk + "s"], [ysk])
                    s.dma("sp", ysc[i * 128:(i + 1) * 128, :], ysb[:, :], reads=[ysk])
                s.flush()
        if nstages < 10:
            return nc
        with ExitStack() as st:
            yT = T(st, "yT", [128, 44, 1028], BF16)
            for k0 in range(0, 44, 11):
                s.dma("sp", yT[:, k0:k0 + 11, :], ysc[k0 * 128:(k0 + 11) * 128, :].rearrange("(kt p) n -> p kt n", p=128),
                      writes=["yT"])
            wdr = Ring(st, nc, "wd", 2, [128, 44, 256], BF16)
            uo = Ring(st, nc, "uo", 3, [6, 512], F32)
            tail_i = [0]

            def tail_step():
                q4 = tail_i[0]
                if q4 >= 22:
                    return
                tail_i[0] += 1
                ps, psk = pf.next()
                TRS([(ps[0:6, j * 128:(j + 1) * 128], u6[:, q4 * 4 + j, :], identf[:, :]) for j in range(4)],
                    ["u6", "identf"], [psk])
                ut, utk = uo.next()
                CP(evac_eng(), ut[:, :], ps[0:6, :], [psk], [utk])
                s.dma("sp", conv_p[:, q4 * 512:(q4 + 1) * 512], ut[0:2, :], reads=[utk])
                s.dma("sp", conv_s[:, 1, q4 * 512:(q4 + 1) * 512], ut[2:6, :], reads=[utk])

            s32 = Ring(st, nc, "s32d", 3, [128, 256], F32)
            nxt = wdr.next()
            wload(w_dn, DFF, 0, 256, nxt[0], nxt[1])
            for bi in range(8):
                wd_, wdk = nxt
                if bi + 1 < 8:
                    nxt = wdr.next()
                    wload(w_dn, DFF, (bi + 1) * 256, 256, nxt[0], nxt[1])
                for f in range(9):
                    rows = 128 if f < 8 else 4
                    ps, psk = pf.next()
                    MM(ps[0:rows, 0:256], [(yT[:, kt, f * 128:f * 128 + rows], wd_[:, kt, :]) for kt in range(44)],
                       [wdk, "yT"], [psk])
                    sb, sbk = s32.next()
                    CP(evac_eng(), sb[0:rows, :], ps[0:rows, 0:256], [psk], [sbk])
                    s.dma("sp", fsc[f * 128:f * 128 + rows, bi * 256:(bi + 1) * 256], sb[0:rows, :], reads=[sbk])
                    tail_step()
            s.flush()
        if nstages < 11:
            return nc
        with ExitStack() as st:
            gpo = T(st, "gpo", [128, D], F32)
            s.dma("sp", gpo[:], g_po[0].partition_broadcast(128), writes=["gpo"])
            fr = Ring(st, nc, "ft", 3, [128, D], F32); xr = Ring(st, nc, "x1h", 3, [128, D], F32)
            junk = T(st, "junkH", [128, D], BF16)

            def ldH(f):
                rows = 128 if f < 8 else 4
                ft, fk = fr.next(); xt, xk = xr.next()
                s.dma("sp", ft[0:rows, :], fsc[f * 128:f * 128 + rows, :], writes=[fk])
                s.dma("sp", xt[0:rows, :], x1[(f + 1) * 128:(f + 1) * 128 + rows, :], writes=[xk])
                return ft, fk, xt, xk

            LH = {0: ldH(0), 1: ldH(1)}
            for f in range(9):
                rows = 128 if f < 8 else 4
                ft, fk, xt, xk = LH.pop(f)
                if f + 2 < 9:
                    LH[f + 2] = ldH(f + 2)
                col, ck_ = rms_col(ft, fk, rows, junk)
                STT(ft[0:rows, :], ft[0:rows, :], col, gpo[0:rows, :], ALU.mult, ALU.mult, [fk, ck_, "gpo"], [fk])
                TT(ft[0:rows, :], ft[0:rows, :], xt[0:rows, :], ALU.add, [fk, xk], [fk])
                s.dma("sp", y_o[f * 128:(f + 1) * 128, :] if f < 8 else ys_o, ft[0:rows, :], reads=[fk])
            s.flush()
    return nc


def _rope_table(pos):
    half = 8
    inv = (np.float32(500000.0) ** (-np.arange(half, dtype=np.float32) * np.float32(2.0) / np.float32(16.0))).astype(np.float32)
    ang = pos.astype(np.float32)[:, None] * inv[None, :]
    return np.concatenate([np.cos(ang), np.sin(ang)], axis=1).astype(np.float32)


def make_in_maps(inp):
    f = lambda a: np.ascontiguousarray(np.asarray(a, dtype=np.float32))
    xp = f(inp["x_prompt"]); xsm = f(inp["x_sample"])
    shared = {
        "identf": np.eye(128, dtype=np.float32),
        "maskT": np.triu(np.ones((64, 64), dtype=np.float32)),
        "g_pre": f(inp["g_pre_mix"]).reshape(1, D), "w_in": f(inp["w_in"])[0], "b_if": f(inp["b_if"]).reshape(1, 8),
        "sinks": f(inp["attn_sinks"]).reshape(1, 32), "w_a": f(inp["w_branch_a"])[0], "w_b": f(inp["w_branch_b"])[0],
        "w_o": f(inp["w_out"])[0], "g_pm": f(inp["g_post_mix"]).reshape(1, D), "g_pf": f(inp["g_pre_ffn"]).reshape(1, D),
        "w_up": f(inp["w_up"])[0], "w_cv": f(inp["w_conv"])[0], "b_cv": f(inp["b_conv"]).reshape(1, 2 * DFF),
        "w_dn": f(inp["w_down"])[0], "g_po": f(inp["g_post_ffn"]).reshape(1, D),
    }
    qi = np.arange(128)[:, None]; kk = np.arange(256)[None, :]
    band = (kk >= qi) & (kk <= qi + 128)
    m_norm = np.where(band, 0.0, -30000.0).astype(np.float32)
    m_first = np.where(band & (kk >= 128), 0.0, -30000.0).astype(np.float32)
    maps = []
    for c in range(8):
        b, hf = c // 2, c % 2
        T0 = hf * 1024
        xw = np.zeros((2048, D), np.float32)
        lo = T0 - 1024
        if lo < 0:
            xw[-lo:] = xp[b, 0:T0 + 1024]
        else:
            xw[:] = xp[b, lo:T0 + 1024]
        pos = np.arange(lo, T0 + 1024)
        valid = (pos >= 0).astype(np.float32)
        ropet = np.concatenate([_rope_table(np.maximum(pos, 0)), _rope_table(np.full(4, 16384))], axis=0)
        sl = slice(4 * c, 4 * c + 4)
        m = dict(shared)
        m.update({
            "xw": xw, "xs": np.ascontiguousarray(xsm[sl, 0, :]),
            "gvalid": np.ascontiguousarray(np.broadcast_to(valid[None, :], (4, 2048))),
            "masks": np.ascontiguousarray(np.stack([m_norm, m_norm if hf == 1 else m_first])),
            "ropet": np.ascontiguousarray(ropet),
            "st_c": f(inp["state_mlstm_c"])[0, sl], "st_n": f(inp["state_mlstm_n"])[0, sl], "st_m": f(inp["state_mlstm_m"])[0, sl],
            "ck": f(inp["cache_swa_k"])[0, sl].reshape(4, 128, 256), "cv": f(inp["cache_swa_v"])[0, sl].reshape(4, 128, 256),
            "cst": f(inp["state_ffn_conv"])[0, sl].reshape(8, 2 * DFF),
        })
        maps.append({k: np.ascontiguousarray(v) for k, v in m.items()})
    return maps


def assemble(res):
    r = res
    yp = np.zeros((4, 2048, D), np.float32); ysm = np.zeros((32, 1, D), np.float32)
    cp = np.zeros((1, 4, 4, 512, 256), np.float32); npp = np.zeros((1, 4, 4, 256), np.float32); mp = np.zeros((1, 4, 4), np.float32)
    kp = np.zeros((1, 4, 128, 4, 64), np.float32); vp = np.zeros((1, 4, 128, 4, 64), np.float32)
    fp = np.zeros((1, 4, 2, 2 * DFF), np.float32)
    cs = np.zeros((1, 32, 4, 512, 256), np.float32); ns = np.zeros((1, 32, 4, 256), np.float32); ms = np.zeros((1, 32, 4), np.float32)
    ks = np.zeros((1, 32, 128, 4, 64), np.float32); vs = np.zeros((1, 32, 128, 4, 64), np.float32)
    fs = np.zeros((1, 32, 2, 2 * DFF), np.float32)
    for c in range(8):
        b, hf = c // 2, c % 2
        o = r[c]
        yp[b, hf * 1024:(hf + 1) * 1024] = o["y"]
        sl = slice(4 * c, 4 * c + 4)
        ysm[sl, 0] = o["ys"]
        if hf == 1:
            cp[0, b] = o["c_p"]; npp[0, b] = o["n_p"]; mp[0, b] = o["m_p"].reshape(4)
            kp[0, b] = o["k_p"].reshape(128, 4, 64); vp[0, b] = o["v_p"].reshape(128, 4, 64); fp[0, b] = o["conv_p"]
        cs[0, sl] = o["c_s"]; ns[0, sl] = o["n_s"]; ms[0, sl] = o["m_s"]
        ks[0, sl] = o["k_s"].reshape(4, 128, 4, 64); vs[0, sl] = o["v_s"].reshape(4, 128, 4, 64); fs[0, sl] = o["conv_s"]
    return (yp, ysm, cp, npp, mp, kp, vp, fp, cs, ns, ms, ks, vs, fs)


def kernel(**inputs):
    nst = int(os.environ.get("KSTAGES", "99"))
    nc = build(nst, False)
    maps = make_in_maps(inputs)
    res = run_bass_kernel_spmd(nc, maps, core_ids=list(range(8)))
    return assemble(res.results)
```

```python
import os
import numpy as np
from contextlib import ExitStack
import concourse.bass as bass
import concourse.mybir as mybir
from concourse.bass_utils import run_bass_kernel_spmd

F32 = mybir.dt.float32
BF16 = mybir.dt.bfloat16
AF = mybir.ActivationFunctionType
ALU = mybir.AluOpType
AX = mybir.AxisListType

EPS = 1e-6
BIG = 1.0e4
D = 2048
DIN = 12808
DFF = 5632
ZC = dict(qm=0, km=1024, vm=2048, om=4096, ig=6144, fg=6148, qa=6152, ka=8200, va=8456, ga=8712, gb=10760)
NFULL = 1152
NFS = NFULL + 4
ENGS = ("pe", "act", "dve", "pool", "sp")


class Sched:
    def __init__(self, nc, n_dma_sems=44, n_pool=12):
        self.nc = nc
        self.q = {e: [] for e in ENGS}
        self.sem = {}
        self.cnt = {e: 0 for e in ENGS}
        self.seen = {e: {} for e in ENGS}
        self.lastw = {}
        self.reads = {}
        self.dma_sems = []
        self.dma_cnt = []
        self.dma_rr = {}
        self.n_dma_sems = n_dma_sems
        self.n_pool = n_pool

    def setup(self, stack):
        nc = self.nc
        for e in ENGS:
            if e != "sp":
                self.sem[e] = stack.enter_context(nc.semaphore("s_" + e))
        for i in range(self.n_dma_sems):
            self.dma_sems.append(stack.enter_context(nc.semaphore("s_dma%d" % i)))
            self.dma_cnt.append(0)

    def _deps(self, eng, reads, writes):
        evs = []
        for r in reads:
            if r in self.lastw:
                evs.append(self.lastw[r])
        for w in writes:
            if w in self.lastw:
                evs.append(self.lastw[w])
            evs.extend(self.reads.get(w, []))
        need = {}
        for (sem, val, src) in evs:
            if src == "pe" and eng == "pe":
                continue
            k = id(sem)
            if self.seen[eng].get(k, 0) >= val:
                continue
            if k not in need or need[k][1] < val:
                need[k] = (sem, val)
        for k, (sem, val) in need.items():
            self.seen[eng][k] = val
        return list(need.values())

    def _commit(self, ev, reads, writes):
        for r in reads:
            self.reads.setdefault(r, []).append(ev)
        for w in writes:
            self.lastw[w] = ev
            self.reads[w] = []

    def _emit_waits(self, eng, waits):
        for (sem, val) in waits:
            self.q[eng].append(lambda e, sem=sem, val=val: e.wait_ge(sem, val))

    def op(self, eng, fn, reads=(), writes=()):
        self._emit_waits(eng, self._deps(eng, reads, writes))
        self.cnt[eng] += 1
        sem = self.sem[eng]
        self.q[eng].append(lambda e, fn=fn, sem=sem: fn(e).then_inc(sem, 1))
        ev = (sem, self.cnt[eng], eng)
        self._commit(ev, reads, writes)
        return ev

    def pe_group(self, fns, reads=(), writes=()):
        eng = "pe"
        self._emit_waits(eng, self._deps(eng, reads, writes))
        self.cnt[eng] += 1
        sem = self.sem[eng]
        for fn in fns[:-1]:
            self.q[eng].append(lambda e, fn=fn: fn(e))
        fn = fns[-1]
        self.q[eng].append(lambda e, fn=fn, sem=sem: fn(e).then_inc(sem, 1))
        ev = (sem, self.cnt[eng], eng)
        self._commit(ev, reads, writes)
        return ev

    def dma(self, eng, out, in_, reads=(), writes=(), **kw):
        self._emit_waits(eng, self._deps(eng, reads, writes))
        lo, hi = (0, self.n_pool) if eng == "pool" else (self.n_pool, self.n_dma_sems)
        i = self.dma_rr.get(eng, lo)
        self.dma_rr[eng] = lo + (i + 1 - lo) % (hi - lo)
        sem = self.dma_sems[i]
        prev = self.dma_cnt[i]
        if prev > 0 and self.seen[eng].get(id(sem), 0) < prev:
            self.q[eng].append(lambda e, sem=sem, prev=prev: e.wait_ge(sem, prev))
            self.seen[eng][id(sem)] = prev
        self.dma_cnt[i] += 16
        val = self.dma_cnt[i]
        self.q[eng].append(
            lambda e, out=out, in_=in_, sem=sem, kw=kw: e.dma_start(out=out, in_=in_, **kw).then_inc(sem, 16))
        ev = (sem, val, "dma")
        self._commit(ev, reads, writes)
        return ev

    def flush(self):
        nc = self.nc
        for i, sem in enumerate(self.dma_sems):
            if self.dma_cnt[i] > 0:
                self.q["sp"].append(lambda e, sem=sem, val=self.dma_cnt[i]: e.wait_ge(sem, val))
        for en in ENGS:
            if en != "sp" and self.cnt[en] > 0:
                self.q["sp"].append(lambda e, sem=self.sem[en], val=self.cnt[en]: e.wait_ge(sem, val))
        q = self.q
        self.q = {e: [] for e in ENGS}
        with nc.Block() as block:
            @block.tensor
            def _(eng):
                for f in q["pe"]:
                    f(eng)

            @block.scalar
            def _(eng):
                for f in q["act"]:
                    f(eng)

            @block.vector
            def _(eng):
                for f in q["dve"]:
                    f(eng)

            @block.gpsimd
            def _(eng):
                for f in q["pool"]:
                    f(eng)

            @block.sync
            def _(eng):
                for f in q["sp"]:
                    f(eng)
        for e in ENGS:
            for en in ENGS:
                if en != "sp":
                    self.seen[e][id(self.sem[en])] = self.cnt[en]
            for i, sem in enumerate(self.dma_sems):
                self.seen[e][id(sem)] = self.dma_cnt[i]
        self.lastw = {}
        self.reads = {}


class Ring:
    def __init__(self, st, nc, name, n, shape, dt, psum=False):
        mk = nc.psum_tensor if psum else nc.sbuf_tensor
        self.t = [st.enter_context(mk("r_%s%d" % (name, i), shape, dt)) for i in range(n)]
        self.k = ["%s%d" % (name, i) for i in range(n)]
        self.i = 0

    def next(self):
        i = self.i
        self.i = (i + 1) % len(self.t)
        return self.t[i], self.k[i]


class RingCat:
    def __init__(self, *rings):
        self.t = [t for r in rings for t in r.t]
        self.k = [k for r in rings for k in r.k]
        self.i = 0

    next = Ring.next


def build(nstages=99, debug=False):
    nc = bass.Bass("TRN2", target_bir_lowering=False)
    s = Sched(nc)

    def din(name, shape):
        return nc.dram_tensor(name, list(shape), F32, kind="ExternalInput").ap()

    def dout(name, shape):
        return nc.dram_tensor(name, list(shape), F32, kind="ExternalOutput").ap()

    def dscr(name, shape, dt):
        return nc.dram_tensor(name, list(shape), dt, kind="ExternalOutput" if debug else "Internal").ap()

    xw = din("xw", [2048, D]); xs = din("xs", [4, D]); gvalid = din("gvalid", [4, 2048])
    masks = din("masks", [2, 128, 256]); ropet = din("ropet", [2052, 16])
    identf_d = din("identf", [128, 128]); maskT_d = din("maskT", [64, 64])
    st_c = din("st_c", [4, 4, 512, 256]); st_n = din("st_n", [4, 4, 256]); st_m = din("st_m", [4, 4])
    ck = din("ck", [4, 128, 256]); cv = din("cv", [4, 128, 256]); cst = din("cst", [8, 11264])
    g_pre = din("g_pre", [1, D]); w_in = din("w_in", [D, DIN]); b_if = din("b_if", [1, 8])
    sinks = din("sinks", [1, 32]); w_a = din("w_a", [D, D]); w_b = din("w_b", [D, D]); w_o = din("w_o", [D, D])
    g_pm = din("g_pm", [1, D]); g_pf = din("g_pf", [1, D]); w_up = din("w_up", [D, 2 * DFF])
    w_cv = din("w_cv", [3, 2 * DFF]); b_cv = din("b_cv", [1, 2 * DFF]); w_dn = din("w_dn", [DFF, D])
    g_po = din("g_po", [1, D])
    y_o = dout("y", [1024, D]); ys_o = dout("ys", [4, D])
    c_p = dout("c_p", [4, 512, 256]); n_p = dout("n_p", [4, 256]); m_p = dout("m_p", [1, 4])
    k_p = dout("k_p", [128, 256]); v_p = dout("v_p", [128, 256]); conv_p = dout("conv_p", [2, 2 * DFF])
    c_s = dout("c_s", [4, 4, 512, 256]); n_s = dout("n_s", [4, 4, 256]); m_s = dout("m_s", [4, 4])
    k_s = dout("k_s", [4, 128, 256]); v_s = dout("v_s", [4, 128, 256]); conv_s = dout("conv_s", [4, 2, 2 * DFF])
    zk = dscr("zk", [2048, 1024], BF16); zv = dscr("zv", [2048, 2048], BF16)
    zo = dscr("zo", [NFULL, 2048], BF16)
    zqT = dscr("zqT", [1024, NFULL], BF16); zkT = dscr("zkT", [1024, NFULL], BF16)
    zs = dscr("zs", [4, DIN], F32)
    qaT = dscr("qaT", [2048, NFULL], BF16); kaT = dscr("kaT", [256, 1280], BF16); va = dscr("va", [1280, 256], BF16)
    sga = dscr("sga", [2048, NFS], BF16); sgb = dscr("sgb", [2048, NFS], BF16)
    hm = dscr("hm", [NFULL, 2048], BF16); ha = dscr("ha", [NFS, 2048], BF16)
    mo = dscr("mo", [NFS, 2048], F32); x1 = dscr("x1", [NFS, 2048], F32)
    ysc = dscr("ysc", [DFF, 1028], BF16); fsc = dscr("fsc", [1028, 2048], F32)
    dsc = dscr("dsc", [4, 32], F32); sscr = dscr("sscr", [4, 12], F32); vscr = dscr("vscr", [4, 256], BF16)

    def ACT(out, in_, func, r, w, **kw):
        s.op("act", lambda e: e.activation(out=out, in_=in_, func=func, **kw), r, w)

    def TT(out, in0, in1, op, r, w, eng="dve"):
        s.op(eng, lambda e: e.tensor_tensor(out=out, in0=in0, in1=in1, op=op), r, w)

    def TS(out, in0, s1, op0, r, w, s2=None, op1=None, eng="dve"):
        if op1 is None:
            s.op(eng, lambda e: e.tensor_scalar(out=out, in0=in0, scalar1=s1, scalar2=None, op0=op0), r, w)
        else:
            s.op(eng, lambda e: e.tensor_scalar(out=out, in0=in0, scalar1=s1, scalar2=s2, op0=op0, op1=op1), r, w)

    def STT(out, in0, sc, in1, op0, op1, r, w, accum=None):
        if accum is None:
            s.op("dve", lambda e: e.scalar_tensor_tensor(out=out, in0=in0, scalar=sc, in1=in1, op0=op0, op1=op1), r, w)
        else:
            s.op("dve", lambda e: e.scalar_tensor_tensor(out=out, in0=in0, scalar=sc, in1=in1, op0=op0, op1=op1,
                                                         accum_out=accum), r, w)

    def CP(eng, out, in_, r, w):
        if eng == "act":
            s.op("act", lambda e: e.copy(out=out, in_=in_), r, w)
        else:
            s.op(eng, lambda e: e.tensor_copy(out=out, in_=in_), r, w)

    def RECIP(out, in_, r, w):
        s.op("dve", lambda e: e.reciprocal(out=out, in_=in_), r, w)

    def MEMSET(ap, val, w, eng="dve"):
        s.op(eng, lambda e: e.memset(ap, val), (), w)

    def SCAN(out, d0, d1, init, op0, op1, r, w):
        s.op("dve", lambda e: e.tensor_tensor_scan(out=out, data0=d0, data1=d1, initial=init, op0=op0, op1=op1), r, w)

    def MM(ps, pairs, r, w):
        n = len(pairs)
        fns = []
        for i, (l, rr) in enumerate(pairs):
            fns.append(lambda e, l=l, rr=rr, i=i: e.matmul(ps, lhsT=l, rhs=rr, start=(i == 0), stop=(i == n - 1)))
        s.pe_group(fns, r, w)

    def TRS(items, r, w):
        fns = [(lambda e, o=o, i=i, d=d: e.transpose(o, i, d)) for (o, i, d) in items]
        s.pe_group(fns, r, w)

    evac_rr = [0]

    def evac_eng():
        evac_rr[0] ^= 1
        return "act" if evac_rr[0] else "dve"

    with ExitStack() as g:
        s.setup(g)

        def T(st, name, shape, dt):
            return st.enter_context(nc.sbuf_tensor("t_" + name, shape, dt))

        pf4 = Ring(g, nc, "pf", 4, [128, 512], F32, psum=True)
        pfs = Ring(g, nc, "pfs", 2, [128, 512], F32, psum=True)
        pf = RingCat(pf4, pfs)
        pb = Ring(g, nc, "pb", 2, [128, 1024], BF16, psum=True)
        identf = T(g, "identf", [128, 128], F32)
        identb = T(g, "identb", [128, 128], BF16)
        cst_t = T(g, "cst_t", [128, 4], F32)
        ones_bf = T(g, "ones_bf", [128, 2], BF16)
        s.dma("sp", identf[:], identf_d, writes=["identf"])
        CP("dve", identb[:], identf[:], ["identf"], ["identb"])
        MEMSET(cst_t[:, 0:1], 1.0, ["cst"]); MEMSET(cst_t[:, 1:2], 0.0, ["cst"]); MEMSET(cst_t[:, 2:3], EPS, ["cst"])
        MEMSET(ones_bf[:], 1.0, ["ones_bf"])
        one_c = cst_t[:, 0:1]
        skc = T(g, "skc", [64, 128], F32); thrc = T(g, "thrc", [64, 128], F32); decb = T(g, "decb", [128, 128], F32)
        hmTs = T(g, "hmTs", [128, 16, 4], BF16)
        wcT = T(g, "wcT", [128, 88, 4], F32)
        pastT = T(g, "pastT", [128, 88, 8], F32)
        u6 = T(g, "u6", [128, 88, 6], F32)
        statr = Ring(g, nc, "stat", 6, [128, 4], F32)

        def rms_col(src, srck, rows, junk):
            sq, sk_ = statr.next()
            ACT(junk[0:rows, :], src[0:rows, :], AF.Square, [srck], [sk_], accum_out=sq[0:rows, 0:1])
            TS(sq[0:rows, 1:2], sq[0:rows, 0:1], 1.0 / D, ALU.mult, [sk_], [sk_], s2=EPS, op1=ALU.add)
            ACT(sq[0:rows, 2:3], sq[0:rows, 1:2], AF.Sqrt, [sk_], [sk_])
            RECIP(sq[0:rows, 3:4], sq[0:rows, 2:3], [sk_], [sk_])
            return sq[0:rows, 3:4], sk_

        def transpose_to(src, srck, rows, ntile, dst, dstk, tok0):
            for kq in range(0, ntile, 4):
                nj = min(4, ntile - kq)
                pt, pk = pb.next()
                TRS([(pt[:, j * 128:j * 128 + rows], src[0:rows, (kq + j) * 128:(kq + j + 1) * 128],
                      identb[0:rows, 0:rows]) for j in range(nj)], [srck, "identb"], [pk])
                CP(evac_eng(), dst[:, kq:kq + nj, tok0:tok0 + rows],
                   pt[:, 0:nj * 128].rearrange("p (j c) -> p j c", c=128)[:, :, 0:rows], [pk], [dstk])

        def rope(x3, cs, rows, nh, tmp, key):
            x1_ = x3[:, :, 0:8]; x2_ = x3[:, :, 8:16]
            cos = cs[0:rows, 0:8].unsqueeze(1).broadcast_to([rows, nh, 8])
            sin = cs[0:rows, 8:16].unsqueeze(1).broadcast_to([rows, nh, 8])
            t = [tmp[0:rows, i, 0:nh, :] for i in range(4)]
            TT(t[0], x1_, cos, ALU.mult, [key, "ropes"], ["ropetmp"])
            TT(t[1], x2_, sin, ALU.mult, [key, "ropes"], ["ropetmp"])
            TT(t[2], x2_, cos, ALU.mult, [key, "ropes"], ["ropetmp"])
            TT(t[3], x1_, sin, ALU.mult, [key, "ropes"], ["ropetmp"])
            TT(x1_, t[0], t[1], ALU.subtract, ["ropetmp"], [key])
            TT(x2_, t[2], t[3], ALU.add, ["ropetmp"], [key])

        def wload(W, K, c0, n, wt, wk, col_off=0):
            KT = K // 128
            for k0 in range(0, KT, 8):
                k1 = min(KT, k0 + 8)
                s.dma("pool", wt[:, k0:k1, col_off:col_off + n],
                      W[k0 * 128:k1 * 128, c0:c0 + n].rearrange("(kt p) n -> p kt n", p=128), writes=[wk])

        def attn_unit(M, qT_ap, kT_ap, NK, vtiles, mask_ap, sink_ap, out_ap, rkeys, outk, R):
            smr, pr, ptr, ar = R
            ps, psk = pf.next()
            MM(ps[0:M, 0:NK], [(qT_ap, kT_ap)], rkeys, [psk])
            sm, smk = smr.next()
            if mask_ap is not None:
                STT(sm[0:M, 0:NK], ps[0:M, 0:NK], 0.125, mask_ap, ALU.mult, ALU.add, [psk, "masks"], [smk])
            else:
                TS(sm[0:M, 0:NK], ps[0:M, 0:NK], 0.125, ALU.mult, [psk], [smk])
            a, ak = ar.next()
            s.op("dve", lambda e: e.reduce_max(out=a[0:M, 0:1], in_=sm[0:M, 0:NK], axis=AX.X), [smk], [ak])
            TS(a[0:M, 1:2], a[0:M, 0:1], sink_ap, ALU.max, [ak, "sinks"], [ak], s2=-1.0, op1=ALU.mult)
            p, pk = pr.next()
            ACT(p[0:M, 0:NK], sm[0:M, 0:NK], AF.Exp, [smk, ak], [pk, ak + "s"], bias=a[0:M, 1:2], accum_out=a[0:M, 2:3])
            ACT(a[0:M, 3:4], sink_ap, AF.Exp, [ak, "sinks"], [ak + "e"], bias=a[0:M, 1:2])
            TT(a[0:M, 4:5], a[0:M, 2:3], a[0:M, 3:4], ALU.add, [ak + "s", ak + "e"], [ak + "d"])
            RECIP(a[0:M, 5:6], a[0:M, 4:5], [ak + "d"], [ak + "d"])
            pt, ptk = pb.next()
            items = []
            off = 0
            for i, (v_ap, nk) in enumerate(vtiles):
                items.append((pt[0:nk, i * 128:i * 128 + M], p[0:M, off:off + nk], identb[0:M, 0:M]))
                off += nk
            TRS(items, [pk, "identb"], [ptk])
            pT, pTk = ptr.next()
            if M == 128 and all(nk == 128 for _, nk in vtiles):
                CP(evac_eng(), pT[:, 0:len(vtiles), :],
                   pt[:, 0:len(vtiles) * 128].rearrange("p (j c) -> p j c", c=128), [ptk], [pTk])
            else:
                for i, (v_ap, nk) in enumerate(vtiles):
                    CP(evac_eng(), pT[0:nk, i, 0:M], pt[0:nk, i * 128:i * 128 + M], [ptk], [pTk])
            po, pok = pf.next()
            MM(po[0:M, 0:64], [(pT[0:nk, i, 0:M], v_ap) for i, (v_ap, nk) in enumerate(vtiles)], [pTk] + rkeys, [pok])
            ACT(out_ap, po[0:M, 0:64], AF.Copy, [pok, ak + "d"], [outk], scale=a[0:M, 5:6])

        FULLCH = [(896, 512), (1408, 512), (1920, 128)]

        def hkeys(t0, nt):
            return ["hT%d" % t for t in range(t0 // 128, (t0 + nt - 1) // 128 + 1)]

        gH = ExitStack()
        hT = T(gH, "hT", [128, 16, 2052], BF16)
        with ExitStack() as gB:
            g_ig = T(gB, "g_ig", [4, 2048], F32); g_fg = T(gB, "g_fg", [4, 2048], F32)
            stA = ExitStack()
            if True:
                st = stA
                xr = Ring(st, nc, "xa", 5, [128, D], F32)
                hnr = Ring(st, nc, "hn", 3, [128, D], BF16)
                junk = T(st, "junkA", [128, D], BF16)
                gbc = T(st, "gpre", [128, D], F32)
                s.dma("sp", gbc[:], g_pre[0].partition_broadcast(128), writes=["gpre"])
                def gen_A():
                    order = [16] + list(range(16))
                    loaded = {}

                    def ldA(t):
                        rows = 128 if t < 16 else 4
                        src = xw[t * 128:(t + 1) * 128, :] if t < 16 else xs
                        xt, xk = xr.next()
                        s.dma("sp", xt[0:rows, :], src, writes=[xk])
                        loaded[t] = (xt, xk)

                    for t in order[0:3]:
                        ldA(t)
                    def a1(idx):
                        t = order[idx]
                        if idx + 3 < len(order):
                            ldA(order[idx + 3])
                        rows = 128 if t < 16 else 4
                        xt, xk = loaded.pop(t)
                        col, ck_ = rms_col(xt, xk, rows, junk)
                        hn, hk = hnr.next()
                        STT(hn[0:rows, :], xt[0:rows, :], col, gbc[0:rows, :], ALU.mult, ALU.mult, [xk, ck_, "gpre"], [hk])
                        return t, rows, hn, hk

                    pend = a1(0)
                    for idx in range(len(order)):
                        cur = pend
                        if idx + 1 < len(order):
                            pend = a1(idx + 1)
                        t, rows, hn, hk = cur
                        transpose_to(hn, hk, rows, 16, hT, "hT%d" % t, t * 128)
                        yield

                gA = gen_A()
                for _ in range(3):
                    next(gA, None)
                if nstages < 2:
                    for _ in gA:
                        pass
            if nstages < 2:
                s.flush()
                stA.close()
                return nc
            with ExitStack() as st:
                wr = Ring(st, nc, "win", 2, [128, 16, 512], BF16)
                stg = Ring(st, nc, "stgB", 4, [128, 512], BF16)
                stgT = Ring(st, nc, "stgT", 2, [128, 4, 128], BF16)
                qfr = Ring(st, nc, "qf", 4, [128, 512], F32)
                zsr = Ring(st, nc, "zsb", 2, [4, 512], F32)
                rtmp = T(st, "ropetmp", [128, 4, 8, 8], F32)
                rcs = T(st, "ropes", [128, 17, 16], F32)
                s.dma("sp", rcs[:, 0:16, :], ropet[0:2048, :].rearrange("(t p) c -> p t c", p=128), writes=["ropes"])
                s.dma("sp", rcs[0:4, 16, :], ropet[2048:2052, :], writes=["ropes"])
                blocks = []
                for i in range(4): blocks.append((ZC["vm"] + 512 * i, 512, "vm", i))
                for i in range(2): blocks.append((ZC["km"] + 512 * i, 512, "km", i))
                blocks.append((ZC["ig"], 8, "gt", 0))
                for i in range(2): blocks.append((ZC["qm"] + 512 * i, 512, "qm", i))
                for i in range(4): blocks.append((ZC["om"] + 512 * i, 512, "om", i))
                for i in range(4): blocks.append((ZC["qa"] + 512 * i, 512, "qa", i))
                blocks.append((ZC["ka"], 512, "kv", 0))

                def form_a(wt, wk, n, t):
                    rows = 128 if t < 16 else 4
                    ps, psk = pf.next()
                    MM(ps[0:rows, 0:n], [(hT[:, kt, t * 128:t * 128 + rows], wt[:, kt, 0:n]) for kt in range(16)],
                       [wk, "hT%d" % t], [psk])
                    return ps, psk

                def form_b(wt, wk, m0, mrows, t0, nt):
                    ps, psk = pf.next()
                    MM(ps[0:mrows, 0:nt], [(wt[:, kt, m0:m0 + mrows], hT[:, kt, t0:t0 + nt]) for kt in range(16)],
                       [wk] + hkeys(t0, nt), [psk])
                    return ps, psk

                def store_tm(ps, psk, dst, func=None):
                    sb, sbk = stg.next()
                    if func is None:
                        CP(evac_eng(), sb[:, :], ps[:, :], [psk], [sbk])
                    else:
                        ACT(sb[:, :], ps[:, :], func, [psk], [sbk])
                    s.dma("sp", dst, sb[:, :], reads=[sbk])

                def jobs(bi, wt, wk):
                    c0, n, kind, i = blocks[bi]
                    ps, psk = form_a(wt, wk, n, 16)
                    zb, zbk = zsr.next()
                    CP("act", zb[0:4, 0:n], ps[0:4, 0:n], [psk], [zbk])
                    s.dma("sp", zs[:, c0:c0 + n], zb[0:4, 0:n], reads=[zbk])
                    if kind == "km":
                        for t in range(16):
                            ps, psk = form_a(wt, wk, n, t)
                            store_tm(ps, psk, zk[t * 128:(t + 1) * 128, i * 512:(i + 1) * 512])
                    if kind in ("qm", "km", "ga", "gb"):
                        dstT = {"qm": zqT, "km": zkT, "ga": sga, "gb": sgb}[kind]
                        for m in range(4):
                            for (t0, nt) in FULLCH:
                                ps, psk = form_b(wt, wk, m * 128, 128, t0, nt)
                                sb, sbk = stg.next()
                                if kind in ("ga", "gb"):
                                    ACT(sb[:, 0:nt], ps[:, 0:nt], AF.Sigmoid, [psk], [sbk])
                                else:
                                    CP(evac_eng(), sb[:, 0:nt], ps[:, 0:nt], [psk], [sbk])
                                s.dma("sp", dstT[(i * 4 + m) * 128:(i * 4 + m + 1) * 128, t0 - 896:t0 - 896 + nt],
                                      sb[:, 0:nt], reads=[sbk])
                    if kind == "vm":
                        for t in range(16):
                            if i == 0:
                                next(gA, None)
                            ps, psk = form_a(wt, wk, n, t)
                            store_tm(ps, psk, zv[t * 128:(t + 1) * 128, i * 512:(i + 1) * 512])
                    if kind == "om":
                        for t in range(7, 16):
                            ps, psk = form_a(wt, wk, n, t)
                            store_tm(ps, psk, zo[(t - 7) * 128:(t - 6) * 128, i * 512:(i + 1) * 512], AF.Sigmoid)
                    if kind == "gt":
                        for gi, gt_ in enumerate((g_ig, g_fg)):
                            for t0 in range(0, 2048, 512):
                                ps, psk = form_b(wt, wk, gi * 4, 4, t0, 512)
                                CP(evac_eng(), gt_[0:4, t0:t0 + 512], ps[0:4, 0:512], [psk], ["g_gate%d" % gi])
                    if kind in ("qa", "kv"):
                        tiles = list(range(7, 16)) if kind == "qa" else list(range(6, 16))

                        def mm(t):
                            ps, psk = form_a(wt, wk, n, t)
                            qf, qfk = qfr.next()
                            CP("act", qf[:, :], ps[:, :], [psk], [qfk])
                            return t, qf, qfk

                        def post(ctx):
                            t, qf, qfk = ctx
                            if kind == "qa":
                                rope(qf[:, :].rearrange("p (h d) -> p h d", d=64), rcs[:, t, :], 128, 8, rtmp, qfk)
                                sb, sbk = stg.next()
                                CP("act", sb[:, :], qf[:, :], [qfk], [sbk])
                                pt, pk = pb.next()
                                TRS([(pt[:, j * 128:(j + 1) * 128], sb[:, j * 128:(j + 1) * 128], identb[:, :])
                                     for j in range(4)], [sbk, "identb"], [pk])
                                sT, sTk = stgT.next()
                                CP(evac_eng(), sT[:, :, :], pt[:, 0:512].rearrange("p (j c) -> p j c", c=128), [pk], [sTk])
                                s.dma("sp", qaT[i * 512:(i + 1) * 512, (t - 7) * 128:(t - 6) * 128]
                                      .rearrange("(j p) c -> p j c", p=128), sT[:, :, :], reads=[sTk])
                            else:
                                rope(qf[:, 0:256].rearrange("p (h d) -> p h d", d=64), rcs[:, t, :], 128, 4, rtmp, qfk)
                                if t == 15:
                                    s.dma("sp", k_p, qf[:, 0:256], reads=[qfk])
                                    s.dma("sp", v_p, qf[:, 256:512], reads=[qfk])
                                sb, sbk = stg.next()
                                CP("act", sb[:, :], qf[:, :], [qfk], [sbk])
                                s.dma("sp", va[(t - 6) * 128:(t - 5) * 128, :], sb[:, 256:512], reads=[sbk])
                                pt, pk = pb.next()
                                TRS([(pt[:, j * 128:(j + 1) * 128], sb[:, j * 128:(j + 1) * 128], identb[:, :])
                                     for j in range(2)], [sbk, "identb"], [pk])
                                sT, sTk = stgT.next()
                                CP(evac_eng(), sT[:, 0:2, :], pt[:, 0:256].rearrange("p (j c) -> p j c", c=128), [pk], [sTk])
                                s.dma("sp", kaT[:, (t - 6) * 128:(t - 5) * 128].rearrange("(j p) c -> p j c", p=128),
                                      sT[:, 0:2, :], reads=[sTk])

                        pend = [mm(tiles[0]), mm(tiles[1])]
                        for idx in range(len(tiles)):
                            if idx + 2 < len(tiles):
                                pend.append(mm(tiles[idx + 2]))
                            post(pend.pop(0))

                nxt = wr.next()
                wload(w_in, D, blocks[0][0], blocks[0][1], nxt[0], nxt[1])
                for bi in range(len(blocks)):
                    cur = nxt
                    if bi + 1 < len(blocks):
                        nxt = wr.next()
                        wload(w_in, D, blocks[bi + 1][0], blocks[bi + 1][1], nxt[0], nxt[1])
                    jobs(bi, cur[0], cur[1])
                s.flush()
            stA.close()
            if nstages < 3:
                return nc
            with ExitStack() as st:
                val = T(st, "val", [4, 2048], F32); ones4 = T(st, "ones4", [4, 2048], F32)
                bif = T(st, "bif", [4, 4], F32)
                e1 = T(st, "e1", [4, 2048], F32); lf = T(st, "lf", [4, 2048], F32); Bc = T(st, "Bc", [4, 2048], F32)
                aa = T(st, "aa", [4, 2048], F32); tmpc = T(st, "tmpc", [4, 2048], F32); Mx = T(st, "Mx", [4, 2048], F32)
                skr = T(st, "skr", [4, 2048], F32); thr_ = T(st, "thr_", [4, 2048], F32)
                Rp = T(st, "Rp", [4, 32], F32); dec = T(st, "dec", [4, 32], F32); mo_ = T(st, "mo_", [4, 2], F32)
                s.dma("sp", val[:], gvalid, writes=["val"])
                s.dma("sp", bif[:, 0:2], b_if[0].rearrange("(two h) -> h two", two=2), writes=["bif"],
                      allow_slow_non_contiguous=True)
                MEMSET(ones4[:], 1.0, ["ones4"])
                TS(bif[:, 2:3], bif[:, 1:2], -1.0, ALU.mult, ["bif"], ["bif2"])
                ACT(e1[:], g_fg[:], AF.Exp, ["bif2"], ["e1"], scale=-1.0, bias=bif[:, 2:3])
                ACT(e1[:], e1[:], AF.Ln, ["e1", "cst"], ["e1"], bias=one_c[0:4, :])
                STT(lf[:], e1[:], -1.0, val[:], ALU.mult, ALU.mult, ["e1", "val"], ["lf"])
                SCAN(Bc[:], ones4[:], lf[:], 0.0, ALU.mult, ALU.add, ["ones4", "lf"], ["Bc"])
                STT(aa[:], g_ig[:], bif[:, 0:1], Bc[:], ALU.add, ALU.subtract, ["bif", "Bc"], ["aa"])
                TS(tmpc[:], val[:], BIG, ALU.mult, ["val"], ["tmpc"], s2=-BIG, op1=ALU.add)
                TT(aa[:], aa[:], val[:], ALU.mult, ["aa", "val"], ["aa"])
                TT(aa[:], aa[:], tmpc[:], ALU.add, ["aa", "tmpc"], ["aa"])
                SCAN(Mx[:], aa[:], aa[:], 0.0, ALU.max, ALU.max, ["aa"], ["Mx"])
                M3 = Mx[:, :].rearrange("p (c l) -> p c l", l=64)
                R = M3[:, :, 63]
                Rb = M3[:, :, 63:64].broadcast_to([4, 32, 64])
                TT(skr[:, :].rearrange("p (c l) -> p c l", l=64), aa[:, :].rearrange("p (c l) -> p c l", l=64), Rb,
                   ALU.subtract, ["aa", "Mx"], ["skr"])
                ACT(skr[:], skr[:], AF.Exp, ["skr"], ["skr"])
                TS(skr[:], skr[:], 0.0625, ALU.mult, ["skr"], ["skr"])
                TT(thr_[:, :].rearrange("p (c l) -> p c l", l=64), Bc[:, :].rearrange("p (c l) -> p c l", l=64), Rb,
                   ALU.add, ["Bc", "Mx"], ["thr_"])
                ACT(thr_[:], thr_[:], AF.Exp, ["thr_"], ["thr_"], scale=-1.0)
                MEMSET(Rp[:, 0:1], 0.0, ["Rp"])
                CP("dve", Rp[:, 1:32], M3[:, 0:31, 63], ["Mx", "Rp"], ["Rp"])
                TT(dec[:], Rp[:], R, ALU.subtract, ["Rp", "Mx"], ["dec"])
                ACT(dec[:], dec[:], AF.Exp, ["dec"], ["dec"])
                TT(mo_[:, 0:1], Bc[:, 2047:2048], Mx[:, 2047:2048], ALU.add, ["Bc", "Mx"], ["mo_"])
                s.dma("sp", m_p.rearrange("o h -> h o"), mo_[:, 0:1], reads=["mo_"])
                for (src, srck, dst, dstk) in ((skr, "skr", skc, "skc"), (thr_, "thr_", thrc, "thrc")):
                    ps, psk = pf.next()
                    TRS([(ps[0:64, c * 4:(c + 1) * 4], src[:, c * 64:(c + 1) * 64], identf[0:4, 0:4]) for c in range(32)],
                        [srck, "identf"], [psk])
                    CP("dve", dst[:, :], ps[0:64, 0:128], [psk], [dstk])
                s.dma("sp", dsc, dec[:], reads=["dec"], writes=["dsc"])
                s.dma("sp", decb[:], dsc.rearrange("h c -> (h c)").partition_broadcast(128), reads=["dsc"], writes=["decb"])
                s.flush()
        if nstages < 4:
            return nc
        with ExitStack() as st:
            S = T(st, "S", [128, 4, 2, 512], F32); nS = T(st, "nS", [128, 4, 2], F32)
            maskT = T(st, "maskT", [64, 64], F32)
            kr = Ring(st, nc, "kch", 2, [64, 1024], BF16); vr = Ring(st, nc, "vch", 2, [64, 2048], BF16)
            kpr = Ring(st, nc, "kp", 2, [64, 1024], BF16)
            qr = Ring(st, nc, "qTc", 2, [128, 8, 64], BF16); ktr = Ring(st, nc, "kTc", 2, [128, 8, 64], BF16)
            sor = Ring(st, nc, "sigo", 2, [64, 2048], BF16); hmr = Ring(st, nc, "hmsb", 2, [64, 2048], BF16)
            cdr = Ring(st, nc, "cdb", 8, [128, 2, 512], BF16); ndr = Ring(st, nc, "ndb", 8, [128, 2], BF16)
            ssr = Ring(st, nc, "stsb", 8, [64, 64], BF16); rsr = Ring(st, nc, "rsd", 8, [64, 4], F32)
            cor = Ring(st, nc, "cout", 2, [128, 4, 256], F32); nout = T(st, "nout", [2, 512], F32)
            s.dma("sp", maskT[:], maskT_d, writes=["maskT"])
            MEMSET(S[:], 0.0, ["S%d" % h for h in range(4)])
            MEMSET(nS[:], 0.0, ["nS%d" % h for h in range(4)])

            def loads(c):
                kc, kk = kr.next(); vc, vk = vr.next()
                s.dma("sp", kc[:], zk[c * 64:(c + 1) * 64, :], writes=[kk])
                s.dma("sp", vc[:], zv[c * 64:(c + 1) * 64, :], writes=[vk])
                res = [kc, kk, vc, vk]
                if c >= 14:
                    fc = c - 14
                    qc, qk = qr.next(); tc_, tk = ktr.next(); so, sok = sor.next()
                    s.dma("sp", qc[:], zqT.rearrange("(a p) n -> p a n", p=128)[:, :, fc * 64:(fc + 1) * 64], writes=[qk])
                    s.dma("sp", tc_[:], zkT.rearrange("(a p) n -> p a n", p=128)[:, :, fc * 64:(fc + 1) * 64], writes=[tk])
                    s.dma("sp", so[:], zo[fc * 64:(fc + 1) * 64, :], writes=[sok])
                    res += [qc, qk, tc_, tk, so, sok]
                return res

            nxt = loads(0)
            for c in range(32):
                cur = nxt
                if c + 1 < 32:
                    nxt = loads(c + 1)
                kc, kk, vc, vk = cur[0:4]
                full = c >= 14
                if full:
                    qc, qk, tc_, tk, so, sok = cur[4:10]
                    hsb, hsk = hmr.next()
                kp, kpk = kpr.next()
                pss, pssk = pfs.next()
                dcols = [decb[:, h * 32 + c:h * 32 + c + 1] for h in range(4)]
                scols = [skc[:, c * 4 + h:c * 4 + h + 1] for h in range(4)]
                for h in range(4):
                    if h % 2 == 0:
                        TS(kp[:, h * 256:(h + 1) * 256], kc[:, h * 256:(h + 1) * 256], scols[h], ALU.mult, [kk, "skc"], [kpk + str(h)])
                    else:
                        ACT(kp[:, h * 256:(h + 1) * 256], kc[:, h * 256:(h + 1) * 256], AF.Copy, [kk, "skc"], [kpk + str(h)],
                            scale=scols[h])
                if full:
                    cds = []
                    for h in range(4):
                        cdb, cdk = cdr.next(); ndb, ndk = ndr.next()
                        ACT(cdb[:, :, :], S[:, h, :, :], AF.Copy, ["S%d" % h, "decb"], [cdk], scale=dcols[h])
                        TS(ndb[:, :], nS[:, h, :], dcols[h], ALU.mult, ["nS%d" % h, "decb"], [ndk])
                        cds.append((cdb, cdk, ndb, ndk))
                    for h in range(4):
                        MM(pss[0:64, h * 64:(h + 1) * 64], [(tc_[:, h * 2 + dt, :], qc[:, h * 2 + dt, :]) for dt in range(2)],
                           [tk, qk], [pssk + "st%d" % h])
                    sss = []
                    for h in range(4):
                        ssb, ssk = ssr.next()
                        STT(ssb[:, :], pss[0:64, h * 64:(h + 1) * 64], scols[h], maskT[:, :], ALU.mult, ALU.mult,
                            [pssk + "st%d" % h, "skc", "maskT"], [ssk])
                        sss.append((ssb, ssk))
                    pns = []
                    for h in range(4):
                        cdb, cdk, ndb, ndk = cds[h]; ssb, ssk = sss[h]
                        pnum, pnk = pf4.next()
                        MM(pnum[0:64, :], [(ssb[:, :], vc[:, h * 512:(h + 1) * 512])] +
                           [(qc[:, h * 2 + dt, :], cdb[:, dt, :]) for dt in range(2)], [ssk, vk, qk, cdk], [pnk])
                        MM(pss[0:64, 256 + h:257 + h], [(ssb[:, :], ones_bf[0:64, 0:1])] +
                           [(qc[:, h * 2 + dt, :], ndb[:, dt:dt + 1]) for dt in range(2)], [ssk, "ones_bf", qk, ndk],
                           [pssk + "dn%d" % h])
                        pns.append((pnum, pnk))
                    for h in range(4):
                        pnum, pnk = pns[h]
                        pdk = pssk + "dn%d" % h
                        pden = pss[0:64, 256 + h:257 + h]
                        rs, rsk = rsr.next()
                        TS(rs[:, 0:1], pden, -1.0, ALU.mult, [pdk], [rsk])
                        STT(rs[:, 2:3], pden, thrc[:, c * 4 + h:c * 4 + h + 1], rs[:, 0:1], ALU.max, ALU.max,
                            [pdk, "thrc", rsk], [rsk])
                        RECIP(rs[:, 1:2], rs[:, 2:3], [rsk], [rsk])
                        STT(hsb[:, h * 512:(h + 1) * 512], pnum[0:64, :], rs[:, 1:2], so[:, h * 512:(h + 1) * 512],
                            ALU.mult, ALU.mult, [pnk, rsk, sok], [hsk + str(h)])
                for hg in range(2):
                    pus = {}
                    for h in (2 * hg, 2 * hg + 1):
                        for dt in range(2):
                            pu, puk = pf4.next()
                            MM(pu[:, :], [(kp[:, h * 256 + dt * 128:h * 256 + (dt + 1) * 128], vc[:, h * 512:(h + 1) * 512])],
                               [kpk + str(h), vk], [puk])
                            pus[(h, dt)] = (pu, puk)
                            MM(pss[:, 264 + h * 2 + dt:265 + h * 2 + dt],
                               [(kp[:, h * 256 + dt * 128:h * 256 + (dt + 1) * 128], ones_bf[0:64, 0:1])],
                               [kpk + str(h), "ones_bf"], [pssk + "un%d_%d" % (h, dt)])
                    for h in (2 * hg, 2 * hg + 1):
                        for dt in range(2):
                            STT(S[:, h, dt, :], S[:, h, dt, :], dcols[h], pus[(h, dt)][0][:, :], ALU.mult, ALU.add,
                                ["S%d" % h, "decb", pus[(h, dt)][1]], ["S%d" % h])
                        STT(nS[:, h, :], nS[:, h, :], dcols[h], pss[:, 264 + h * 2:266 + h * 2], ALU.mult, ALU.add,
                            ["nS%d" % h, "decb", pssk + "un%d_0" % h, pssk + "un%d_1" % h], ["nS%d" % h])
                if full:
                    s.dma("sp", hm[(c - 14) * 64:(c - 13) * 64, :], hsb[:, :], reads=[hsk + str(h) for h in range(4)])
            for h in range(4):
                co, cok = cor.next()
                for half in range(2):
                    ps, psk = pf4.next()
                    TRS([(ps[:, (vv * 2 + dt) * 128:(vv * 2 + dt + 1) * 128],
                          S[:, h, dt, (half * 2 + vv) * 128:(half * 2 + vv + 1) * 128], identf[:, :])
                         for vv in range(2) for dt in range(2)], ["S%d" % h, "identf"], [psk])
                    CP(evac_eng(), co[:, half * 2:half * 2 + 2, :], ps[:, :].rearrange("p (v k) -> p v k", k=256), [psk], [cok])
                s.dma("sp", c_p[h].rearrange("(vt p) k -> p vt k", p=128), co[:, :, :], reads=[cok])
            ps, psk = pf4.next()
            TRS([(ps[0:2, h * 128:(h + 1) * 128], nS[:, h, :], identf[:, :]) for h in range(4)],
                ["nS%d" % h for h in range(4)] + ["identf"], [psk])
            CP("dve", nout[:, :], ps[0:2, :], [psk], ["nout"])
            s.dma("sp", n_p.rearrange("h (dt p) -> dt h p", p=128), nout[:, :].rearrange("d (h p) -> d h p", p=128),
                  reads=["nout"])
            s.flush()
        if nstages < 5:
            return nc
        def gen_B2(st):
            wr2 = Ring(st, nc, "win2", 2, [128, 16, 256], BF16)
            stg2 = Ring(st, nc, "stg2", 4, [128, 512], BF16)
            cols = [(ZC["ga"] + 256 * i, sga, i) for i in range(8)] + [(ZC["gb"] + 256 * i, sgb, i) for i in range(8)]
            CH4 = [(896, 512, 0), (1408, 512, 512), (1920, 128, 1024), (2048, 4, NFULL)]

            def ld(bi):
                wt, wk = wr2.next()
                wload(w_in, D, cols[bi][0], 256, wt, wk)
                return wt, wk

            nxt = ld(0)
            yield
            for bi in range(16):
                wt, wk = nxt
                if bi + 1 < 16:
                    nxt = ld(bi + 1)
                c0, dstT, i = cols[bi]
                for m in range(2):
                    for (t0, nt, o0) in CH4:
                        ps, psk = pf.next()
                        MM(ps[:, 0:nt], [(wt[:, kt, m * 128:(m + 1) * 128], hT[:, kt, t0:t0 + nt]) for kt in range(16)], [wk], [psk])
                        sb, sbk = stg2.next()
                        CP(evac_eng(), sb[:, 0:nt], ps[:, 0:nt], [psk], [sbk])
                        s.dma("sp", dstT[(i * 2 + m) * 128:(i * 2 + m + 1) * 128, o0:o0 + nt], sb[:, 0:nt], reads=[sbk])
                        yield

        def gen_S(st):
            zsm = T(st, "zsm", [4, 8], F32); stm = T(st, "stm", [4, 4], F32); bib = T(st, "bib", [4, 8], F32)
            sc = T(st, "sc", [4, 24], F32)
            scal = T(st, "scal", [4, 12], F32)
            scb = T(st, "scb", [128, 48], F32)
            zsv = T(st, "zsv", [4, 2048], F32)
            vT = T(st, "vT", [128, 32, 4], F32)
            s.dma("sp", zsm[:], zs[:, ZC["ig"]:ZC["ig"] + 8], writes=["zsm"])
            s.dma("sp", stm[:], st_m, writes=["stm"])
            s.dma("sp", bib[:], b_if[0].partition_broadcast(4), writes=["bib"])
            s.dma("sp", zsv[:], zs[:, ZC["vm"]:ZC["vm"] + 2048], writes=["zsv"])
            qs = T(st, "qs", [4, 2048], F32); ks = T(st, "ks", [4, 512], F32)
            rcs = T(st, "ropesS", [4, 16], F32); sinkc = T(st, "sinkc", [8, 4], F32)
            s.dma("sp", qs[:], zs[:, ZC["qa"]:ZC["qa"] + 2048], writes=["qs"])
            s.dma("sp", ks[:], zs[:, ZC["ka"]:ZC["ka"] + 512], writes=["ks"])
            s.dma("sp", rcs[:], ropet[2048:2052, :], writes=["ropes"])
            s.dma("sp", sinkc[:], sinks[0].rearrange("(g j) -> j g", j=8), writes=["sinksS"], allow_slow_non_contiguous=True)
            yield
            TT(sc[:, 0:8], zsm[:, :], bib[:, :], ALU.add, ["zsm", "bib"], ["sc0"])
            ACT(sc[:, 8:12], sc[:, 4:8], AF.Exp, ["sc0"], ["sc1"], scale=-1.0)
            ACT(sc[:, 8:12], sc[:, 8:12], AF.Ln, ["sc1", "cst"], ["sc1"], bias=one_c[0:4, :])
            STT(sc[:, 12:16], sc[:, 8:12], -1.0, stm[:, :], ALU.mult, ALU.add, ["sc1", "stm"], ["sc2"])
            TT(sc[:, 16:20], sc[:, 0:4], sc[:, 12:16], ALU.max, ["sc0", "sc2"], ["sc3"])
            s.dma("sp", m_s, sc[:, 16:20], reads=["sc3"])
            TT(scal[:, 0:4], sc[:, 12:16], sc[:, 16:20], ALU.subtract, ["sc2", "sc3"], ["scal0"])
            TT(scal[:, 4:8], sc[:, 0:4], sc[:, 16:20], ALU.subtract, ["sc0", "sc3"], ["scal1"])
            ACT(scal[:, 0:8], scal[:, 0:8], AF.Exp, ["scal0", "scal1"], ["scal01"])
            TS(scal[:, 4:8], scal[:, 4:8], 0.0625, ALU.mult, ["scal01"], ["scal01"])
            ACT(scal[:, 8:12], sc[:, 16:20], AF.Exp, ["sc3"], ["scal2"], scale=-1.0)
            s.dma("sp", sscr, scal[:, :], reads=["scal01", "scal2"], writes=["sscr"])
            yield
            s.dma("sp", scb[:], sscr.rearrange("j q -> (j q)").partition_broadcast(128), reads=["sscr"], writes=["scb"])
            for half in range(2):
                if half == 1:
                    s.dma("sp", zsv[:], zs[:, ZC["om"]:ZC["om"] + 2048], reads=["zsv"], writes=["zsv"])
                    ACT(zsv[:, :], zsv[:, :], AF.Sigmoid, ["zsv"], ["zsv"])
                ps, psk = pf.next()
                TRS([(ps[:, j * 4:(j + 1) * 4], zsv[:, j * 128:(j + 1) * 128], identf[0:4, 0:4])
                     for j in range(16)], ["zsv", "identf"], [psk])
                CP("dve", vT[:, half * 16:(half + 1) * 16, :], ps[:, 0:64].rearrange("p (j c) -> p j c", c=4), [psk], ["vT"])
            csr = Ring(st, nc, "Cs", 2, [128, 4, 256], F32); cnr = Ring(st, nc, "Cn", 2, [128, 4, 256], F32)
            qbr_ = Ring(st, nc, "qbc", 2, [128, 256], F32); kbr_ = Ring(st, nc, "kbc", 2, [128, 256], F32)
            nbr_ = Ring(st, nc, "nbc", 2, [128, 256], F32)
            smr_ = Ring(st, nc, "ssm", 3, [128, 16], F32)
            junkS = T(st, "junkS", [128, 256], F32)
            def loadsS(u):
                j, h = u // 4, u % 4
                Cs, Csk = csr.next(); qb_, qbk = qbr_.next(); kb_, kbk = kbr_.next(); nb_, nbk = nbr_.next()
                s.dma("sp", Cs[:], st_c[j, h].rearrange("(vt p) k -> p vt k", p=128),
                      writes=[Csk] + [Csk + "v%d" % vt for vt in range(4)])
                s.dma("sp", qb_[:], zs[j, ZC["qm"] + h * 256:ZC["qm"] + (h + 1) * 256].partition_broadcast(128), writes=[qbk])
                s.dma("sp", kb_[:], zs[j, ZC["km"] + h * 256:ZC["km"] + (h + 1) * 256].partition_broadcast(128), writes=[kbk])
                s.dma("sp", nb_[:], st_n[j, h].partition_broadcast(128), writes=[nbk])
                return Cs, Csk, qb_, qbk, kb_, kbk, nb_, nbk

            nxtS = loadsS(0)
            yield
            for u in range(16):
                j, h = u // 4, u % 4
                Cs, Csk, qb_, qbk, kb_, kbk, nb_, nbk = nxtS
                if u + 1 < 16:
                    nxtS = loadsS(u + 1)
                if True:
                    Cn, Cnk = cnr.next()
                    sm_, smk = smr_.next()
                    dcol = scb[:, j * 12 + h:j * 12 + h + 1]
                    kcol = scb[:, j * 12 + 4 + h:j * 12 + 4 + h + 1]
                    tcol = scb[:, j * 12 + 8 + h:j * 12 + 8 + h + 1]
                    TS(sm_[:, 0:4], vT[:, h * 4:(h + 1) * 4, j], kcol, ALU.mult, ["vT", "scb"], [smk])
                    for vt in range(4):
                        ACT(Cs[:, vt, :], Cs[:, vt, :], AF.Copy, [Csk, "scb"], [Csk + "v%d" % vt], scale=dcol)
                    for vt in range(4):
                        STT(Cn[:, vt, :], kb_[:, :], sm_[:, vt:vt + 1], Cs[:, vt, :], ALU.mult, ALU.add,
                            [kbk, smk, Csk + "v%d" % vt], [Cnk + "v%d" % vt])
                        STT(junkS[:, :], Cn[:, vt, :], 1.0, qb_[:, :], ALU.mult, ALU.mult, [Cnk + "v%d" % vt, qbk], [smk + "n%d" % vt],
                            accum=sm_[:, 4 + vt:5 + vt])
                    s.dma("sp", c_s[j, h].rearrange("(vt p) k -> p vt k", p=128), Cn[:, :, :],
                          reads=[Cnk + "v%d" % vt for vt in range(4)])
                    TS(nb_[:, :], nb_[:, :], dcol, ALU.mult, [nbk, "scb"], [nbk])
                    STT(nb_[:, :], kb_[:, :], kcol, nb_[:, :], ALU.mult, ALU.add, [kbk, nbk, "scb"], [nbk])
                    s.dma("sp", n_s[j, h:h + 1, :], nb_[0:1, :], reads=[nbk])
                    STT(junkS[:, :], nb_[:, :], 1.0, qb_[:, :], ALU.mult, ALU.mult, [nbk, qbk], [smk + "d"], accum=sm_[:, 8:9])
                    TS(sm_[:, 9:10], sm_[:, 8:9], -1.0, ALU.mult, [smk + "d"], [smk + "d"])
                    STT(sm_[:, 11:12], sm_[:, 8:9], tcol, sm_[:, 9:10], ALU.max, ALU.max, [smk + "d", "scb"], [smk + "d"])
                    RECIP(sm_[:, 10:11], sm_[:, 11:12], [smk + "d"], [smk + "d"])
                    STT(hmTs[:, h * 4:(h + 1) * 4, j], sm_[:, 4:8], sm_[:, 10:11], vT[:, 16 + h * 4:16 + (h + 1) * 4, j],
                        ALU.mult, ALU.mult, [smk + "n%d" % vt for vt in range(4)] + [smk + "d", "vT"], ["hmTs"])
                yield
            qsb = T(st, "qsb", [4, 2048], BF16); ksb = T(st, "ksb", [4, 512], BF16)
            rtmp = T(st, "ropetmpS", [4, 4, 32, 8], F32)
            qTs = T(st, "qTs", [64, 32, 4], BF16); knT = T(st, "knT", [64, 4, 4], BF16)
            vnew = T(st, "vnew", [1, 4, 256], BF16)
            hos = T(st, "hos", [8, 4, 4, 64], BF16)
            rope(qs[:, :].rearrange("p (h d) -> p h d", d=64), rcs, 4, 32, rtmp, "qs")
            rope(ks[:, 0:256].rearrange("p (h d) -> p h d", d=64), rcs, 4, 4, rtmp, "ks")
            s.dma("sp", k_s[:, 127, :], ks[:, 0:256], reads=["ks"])
            s.dma("sp", v_s[:, 127, :], ks[:, 256:512], reads=["ks"])
            s.dma("sp", k_s[:, 0:127, :], ck[:, 1:128, :])
            s.dma("sp", v_s[:, 0:127, :], cv[:, 1:128, :])
            CP("act", qsb[:], qs[:], ["qs"], ["qsb"])
            CP("act", ksb[:], ks[:], ["ks"], ["ksb"])
            s.dma("sp", vscr, ksb[:, 256:512], reads=["ksb"], writes=["vscr"])
            yield
            s.dma("sp", vnew[:], vscr.rearrange("(o j) c -> o j c", o=1), reads=["vscr"], writes=["vnew"])
            pt, pk = pb.next()
            TRS([(pt[0:64, hd * 4:(hd + 1) * 4], qsb[:, hd * 64:(hd + 1) * 64], identb[0:4, 0:4]) for hd in range(32)],
                ["qsb", "identb"], [pk])
            CP("dve", qTs[:, :, :], pt[0:64, 0:128].rearrange("p (h c) -> p h c", c=4), [pk], ["qTs"])
            pt, pk = pb.next()
            TRS([(pt[0:64, gq * 4:(gq + 1) * 4], ksb[:, gq * 64:(gq + 1) * 64], identb[0:4, 0:4]) for gq in range(4)],
                ["ksb", "identb"], [pk])
            CP("dve", knT[:, :, :], pt[0:64, 0:16].rearrange("p (h c) -> p h c", c=4), [pk], ["knT"])
            kcr = Ring(st, nc, "kcb", 2, [128, 256], BF16); vcr = Ring(st, nc, "vcb", 2, [128, 256], BF16)
            kter = Ring(st, nc, "kTe", 2, [64, 132], BF16)
            R = (Ring(st, nc, "smS", 3, [128, 256], F32), Ring(st, nc, "ppS", 3, [128, 256], BF16),
                 Ring(st, nc, "pTS", 3, [128, 2, 128], BF16), Ring(st, nc, "astS", 4, [128, 8], F32))
            def loadsC(j):
                kcb, kck = kcr.next(); vcb, vck = vcr.next()
                s.dma("pool", kcb[:], ck[j], writes=[kck])
                s.dma("pool", vcb[:], cv[j], writes=[vck])
                return kcb, kck, vcb, vck

            nxtC = loadsC(0)
            yield
            for j in range(4):
                kcb, kck, vcb, vck = nxtC
                if j + 1 < 4:
                    nxtC = loadsC(j + 1)
                for gq in range(4):
                    pt, pk = pb.next()
                    TRS([(pt[0:64, 0:128], kcb[:, gq * 64:(gq + 1) * 64], identb[:, :])], [kck, "identb"], [pk])
                    kTe, kTk = kter.next()
                    CP(evac_eng(), kTe[:, 0:128], pt[0:64, 0:128], [pk], [kTk])
                    CP("dve", kTe[:, 128:129], knT[:, gq, j:j + 1], ["knT", kTk], [kTk])
                    attn_unit(8, qTs[:, gq * 8:(gq + 1) * 8, j], kTe[:, 0:129], 129,
                              [(vcb[:, gq * 64:(gq + 1) * 64], 128), (vnew[0:1, j, gq * 64:(gq + 1) * 64], 1)],
                              None, sinkc[:, gq:gq + 1], hos[:, j, gq, :], ["qTs", kTk, vck, "vnew", "sinksS"], "hos", R)
                    yield
            s.dma("sp", ha[NFULL:NFS, :].rearrange("j (g hh d) -> hh j g d", g=4, hh=8), hos[:, :, :, :], reads=["hos"])
        with ExitStack() as st:
            masks_sb = T(st, "masks", [128, 2, 256], F32)
            maskb = T(st, "maskb", [128, 2, 512], BF16)
            sink_bc = T(st, "sinks", [128, 32], F32); nsink = T(st, "nsinks", [128, 32], F32)
            s.dma("sp", masks_sb[:], masks.rearrange("m p k -> p m k"), writes=["masks"])
            s.dma("sp", sink_bc[:], sinks[0].partition_broadcast(128), writes=["sinks"])
            TS(nsink[:], sink_bc[:], -1.0, ALU.mult, ["sinks"], ["nsinks"])
            for mi_ in range(2):
                for rep in range(2):
                    CP("dve", maskb[:, mi_, rep * 256:(rep + 1) * 256], masks_sb[:, mi_, :], ["masks"], ["maskb"])
            qbr = Ring(st, nc, "qblk", 2, [64, 32, 128], BF16)
            kbr = Ring(st, nc, "kblk", 2, [64, 4, 256], BF16); vbr = Ring(st, nc, "vblk", 2, [128, 2, 256], BF16)
            har = Ring(st, nc, "hasb", 2, [128, 2048], BF16)
            ar = Ring(st, nc, "ast", 8, [128, 16], F32); pr = Ring(st, nc, "pp", 7, [128, 512], BF16)
            ptr = Ring(st, nc, "pT", 4, [128, 4, 128], BF16)

            def loadsE(f):
                qb, qbk = qbr.next(); kb, kbk = kbr.next(); vb, vbk = vbr.next()
                s.dma("sp", qb[:], qaT.rearrange("(h d) t -> d h t", d=64)[:, :, f * 128:(f + 1) * 128], writes=[qbk])
                s.dma("sp", kb[:], kaT.rearrange("(h d) t -> d h t", d=64)[:, :, f * 128:f * 128 + 256], writes=[kbk])
                s.dma("sp", vb[:], va[f * 128:f * 128 + 256, :].rearrange("(kt p) c -> p kt c", p=128), writes=[vbk])
                return qb, qbk, kb, kbk, vb, vbk

            Bk = {0: loadsE(0), 1: loadsE(1)}
            Hs = {}

            def part1(f, hp):
                qb, qbk, kb, kbk, vb, vbk = Bk[f]
                gq = hp // 4
                mi = 1 if f == 1 else 0
                ps, psk = pf.next()
                s.pe_group([
                    lambda e: e.matmul(ps[:, 0:512], lhsT=identb[:, :], rhs=maskb[:, mi, :], start=True, stop=False),
                    lambda e: e.matmul(ps[:, 0:256], lhsT=qb[:, 2 * hp, :], rhs=kb[:, gq, :], start=False, stop=False),
                    lambda e: e.matmul(ps[:, 256:512], lhsT=qb[:, 2 * hp + 1, :], rhs=kb[:, gq, :], start=False, stop=True),
                ], [qbk, kbk, "maskb", "identb"], [psk])
                a, ak = ar.next()
                s.op("dve", lambda e: e.reduce_max(out=a[:, 0:2], in_=ps[:, 0:512].rearrange("p (h k) -> p h k", k=256),
                                                   axis=AX.X), [psk], [ak])
                TS(a[:, 2:4], a[:, 0:2], -0.125, ALU.mult, [ak], [ak])
                TT(a[:, 4:6], a[:, 2:4], nsink[:, 2 * hp:2 * hp + 2], ALU.min, [ak, "nsinks"], [ak])
                TT(a[:, 8:10], sink_bc[:, 2 * hp:2 * hp + 2], a[:, 4:6], ALU.add, [ak, "sinks"], [ak + "x"])
                p, pk = pr.next()
                ACT(p[:, 0:256], ps[:, 0:256], AF.Exp, [psk, ak], [pk, ak + "s0"], scale=0.125, bias=a[:, 4:5],
                    accum_out=a[:, 6:7])
                ACT(p[:, 256:512], ps[:, 256:512], AF.Exp, [psk, ak], [pk, ak + "s1"], scale=0.125, bias=a[:, 5:6],
                    accum_out=a[:, 7:8])
                ACT(a[:, 10:12], a[:, 8:10], AF.Exp, [ak + "x"], [ak + "e"])
                return (f, hp, gq, p, pk, a, ak)

            def part2a(ctx):
                f, hp, gq, p, pk, a, ak = ctx
                pt, ptk = pb.next()
                TRS([(pt[:, (hh * 2 + kt) * 128:(hh * 2 + kt + 1) * 128], p[:, hh * 256 + kt * 128:hh * 256 + (kt + 1) * 128],
                      identb[:, :]) for hh in range(2) for kt in range(2)], [pk, "identb"], [ptk])
                pT, pTk = ptr.next()
                CP("act", pT[:, :, :], pt[:, 0:512].rearrange("p (j c) -> p j c", c=128), [ptk], [pTk])
                return ctx + (pT, pTk)

            def part2b(ctx):
                f, hp, gq, p, pk, a, ak, pT, pTk = ctx
                vb, vbk = Bk[f][4:6]
                if f not in Hs:
                    Hs[f] = har.next()
                hsb, hsk = Hs[f]
                TT(a[:, 12:14], a[:, 6:8], a[:, 10:12], ALU.add, [ak + "s0", ak + "s1", ak + "e"], [ak + "d"])
                RECIP(a[:, 14:16], a[:, 12:14], [ak + "d"], [ak + "d"])
                po, pok = pf.next()
                for hh in range(2):
                    MM(po[:, hh * 64:(hh + 1) * 64], [(pT[:, hh * 2 + kt, :], vb[:, kt, gq * 64:(gq + 1) * 64]) for kt in range(2)],
                       [pTk, vbk], [pok])
                TT(hsb[:, hp * 128:(hp + 1) * 128].rearrange("p (h d) -> p h d", d=64),
                   po[:, 0:128].rearrange("p (h d) -> p h d", d=64),
                   a[:, 14:16].unsqueeze(2).broadcast_to([128, 2, 64]), ALU.mult, [pok, ak + "d"], [hsk + "_%d" % hp])
                if hp == 15:
                    s.dma("sp", ha[f * 128:(f + 1) * 128, :], hsb[:, :], reads=[hsk + "_%d" % q for q in range(16)])
                    if f + 2 < 9:
                        Bk[f + 2] = loadsE(f + 2)

            units = [(f, hp) for f in range(9) for hp in range(16)]
            gS = gen_S(st)
            gB2 = gen_B2(st)
            next(gB2)
            SKEW = 4
            n_u = len(units)
            pend = [part1(*units[i]) for i in range(SKEW)]
            mid = None
            for i in range(n_u + 1):
                if mid is not None:
                    part2b(mid)
                    mid = None
                if i < n_u:
                    mid = part2a(pend.pop(0))
                    if i + SKEW < n_u:
                        pend.append(part1(*units[i + SKEW]))
                if i % 3 == 2:
                    next(gS, None)
                next(gB2, None)
            for _ in gS:
                pass
            for _ in gB2:
                pass
            s.flush()
        gH.close()
        if nstages < 6:
            return nc
        if nstages < 7:
            return nc
        with ExitStack() as gX:
            xn2T = None
            mixT = T(gX, "mixT", [128, 16, NFS], BF16)
            with ExitStack() as st:
                hmT = T(st, "hmT", [128, 16, NFS], BF16); haT = T(st, "haT", [128, 16, NFS], BF16)
                ldc = Ring(st, nc, "ldc", 4, [8, 1408], F32)

                def prep_load(pi):
                    lt, lk = ldc.next()
                    s.dma("sp", lt[0:3, :], w_cv[:, pi * 1408:(pi + 1) * 1408], writes=[lk])
                    s.dma("sp", lt[3:4, :], b_cv[:, pi * 1408:(pi + 1) * 1408], writes=[lk])
                    lt2, lk2 = ldc.next()
                    s.dma("sp", lt2[0:8, :], cst[:, pi * 1408:(pi + 1) * 1408], writes=[lk2])
                    return lt, lk, lt2, lk2

                def prep_piece(pi, ld_):
                    lt, lk, lt2, lk2 = ld_
                    ps, psk = pf.next()
                    TRS([(ps[:, j * 4:(j + 1) * 4], lt[0:4, j * 128:(j + 1) * 128], identf[0:4, 0:4]) for j in range(11)],
                        [lk, "identf"], [psk])
                    CP("dve", wcT[:, pi * 11:(pi + 1) * 11, :], ps[:, 0:44].rearrange("p (j c) -> p j c", c=4), [psk], ["wcT"])
                    ps, psk = pf.next()
                    TRS([(ps[:, j * 8:(j + 1) * 8], lt2[0:8, j * 128:(j + 1) * 128], identf[0:8, 0:8]) for j in range(11)],
                        [lk2, "identf"], [psk])
                    CP("dve", pastT[:, pi * 11:(pi + 1) * 11, :], ps[:, 0:88].rearrange("p (j c) -> p j c", c=8), [psk], ["pastT"])

                s.dma("sp", conv_s[:, 0, :], cst.rearrange("(j r) c -> j r c", r=2)[:, 1, :])
                prep_ld = {0: prep_load(0)}
                prep_i = [0]

                def prep_step():
                    pi = prep_i[0]
                    if pi >= 8:
                        return
                    if pi + 1 < 8:
                        prep_ld[pi + 1] = prep_load(pi + 1)
                    prep_piece(pi, prep_ld.pop(pi))
                    prep_i[0] += 1

                ld = Ring(st, nc, "ldF", 2, [128, 2048], BF16)
                for (src, dstT, nm) in ((hm, hmT, "hmT"), (ha, haT, "haT")):
                    for f in range(10):
                        rows = 128 if f < 9 else 4
                        if nm == "hmT" and f == 9:
                            continue
                        lt, lk = ld.next()
                        s.dma("sp", lt[0:rows, :], src[f * 128:f * 128 + rows, :], writes=[lk])
                        transpose_to(lt, lk, rows, 16, dstT, nm, f * 128)
                        prep_step()
                CP("dve", hmT[:, :, NFULL:NFS], hmTs[:, :, :], ["hmTs", "hmT"], ["hmT"])
                wra = Ring(st, nc, "wa", 2, [128, 16, 256], BF16); wrb = Ring(st, nc, "wb", 2, [128, 16, 256], BF16)
                gar = Ring(st, nc, "gat", 3, [128, 512], BF16); gbr = Ring(st, nc, "gbt", 3, [128, 512], BF16)
                t1r = Ring(st, nc, "t1", 2, [128, 512], F32); t2r = Ring(st, nc, "t2", 2, [128, 512], F32)
                CH = [(0, 512), (512, 512), (1024, NFS - 1024)]

                def ldw(bi):
                    a_, ak_ = wra.next(); b_, bk_ = wrb.next()
                    wload(w_a, D, bi * 256, 256, a_, ak_)
                    wload(w_b, D, bi * 256, 256, b_, bk_)
                    return a_, ak_, b_, bk_

                nxt = ldw(0)
                for bi in range(8):
                    wa_, wak, wb_, wbk = nxt
                    if bi + 1 < 8:
                        nxt = ldw(bi + 1)
                    for m in range(2):
                        mt = bi * 2 + m
                        for (t0, nt) in CH:
                            ga_, gak = gar.next(); gb_, gbk = gbr.next()
                            s.dma("sp", ga_[:, 0:nt], sga[mt * 128:(mt + 1) * 128, t0:t0 + nt], writes=[gak])
                            s.dma("sp", gb_[:, 0:nt], sgb[mt * 128:(mt + 1) * 128, t0:t0 + nt], writes=[gbk])
                            ACT(ga_[:, 0:nt], ga_[:, 0:nt], AF.Sigmoid, [gak], [gak])
                            ACT(gb_[:, 0:nt], gb_[:, 0:nt], AF.Sigmoid, [gbk], [gbk])
                            pa, pak = pf.next()
                            MM(pa[:, 0:nt], [(wa_[:, kt, m * 128:(m + 1) * 128], hmT[:, kt, t0:t0 + nt]) for kt in range(16)],
                               [wak, "hmT"], [pak])
                            pb2, pbk = pf.next()
                            MM(pb2[:, 0:nt], [(wb_[:, kt, m * 128:(m + 1) * 128], haT[:, kt, t0:t0 + nt]) for kt in range(16)],
                               [wbk, "haT"], [pbk])
                            t1, t1k = t1r.next(); t2, t2k = t2r.next()
                            TT(t1[:, 0:nt], pa[:, 0:nt], ga_[:, 0:nt], ALU.mult, [pak, gak], [t1k])
                            TT(t2[:, 0:nt], pb2[:, 0:nt], gb_[:, 0:nt], ALU.mult, [pbk, gbk], [t2k])
                            TT(mixT[:, mt, t0:t0 + nt], t1[:, 0:nt], t2[:, 0:nt], ALU.add, [t1k, t2k], ["mixT"])
                s.flush()
            if nstages < 8:
                return nc
            xn2T = T(gX, "xn2T", [128, 16, NFS], BF16)
            with ExitStack() as st:
                wo_all = T(st, "wo_all", [128, 16, D], BF16)
                for bi in range(4):
                    wload(w_o, D, bi * 512, 512, wo_all, "wo%d" % bi, col_off=bi * 512)
                gpm = T(st, "gpm", [128, D], F32); gpf = T(st, "gpf", [128, D], F32)
                s.dma("sp", gpm[:], g_pm[0].partition_broadcast(128), writes=["gpm"])
                s.dma("sp", gpf[:], g_pf[0].partition_broadcast(128), writes=["gpf"])
                xr = Ring(st, nc, "xt", 2, [128, D], F32); mor = Ring(st, nc, "mot", 2, [128, D], F32)
                x1r = Ring(st, nc, "x1t", 1, [128, D], F32); xnr = Ring(st, nc, "xnt", 1, [128, D], BF16)

                def ldx(f):
                    rows = 128 if f < 9 else 4
                    xt, xk = xr.next()
                    s.dma("sp", xt[0:rows, :], xw[(f + 7) * 128:(f + 8) * 128, :] if f < 9 else xs, writes=[xk])
                    return xt, xk

                def mmF(f):
                    rows = 128 if f < 9 else 4
                    mt_, mk = mor.next()
                    pss_ = []
                    for bi in range(4):
                        ps, psk = pf.next()
                        MM(ps[0:rows, :], [(mixT[:, kt, f * 128:f * 128 + rows], wo_all[:, kt, bi * 512:(bi + 1) * 512])
                                           for kt in range(16)], ["wo%d" % bi, "mixT"], [psk])
                        pss_.append((ps, psk))
                    return mt_, mk, pss_, rows

                def evF(ctx):
                    mt_, mk, pss_, rows = ctx
                    for bi, (ps, psk) in enumerate(pss_):
                        CP(evac_eng(), mt_[0:rows, bi * 512:(bi + 1) * 512], ps[0:rows, :], [psk], [mk + "_%d" % bi])

                nxt = ldx(0)
                nmm = mmF(0)
                evF(nmm)
                for f in range(10):
                    rows = 128 if f < 9 else 4
                    xt, xk = nxt
                    mt_, mk = nmm[0:2]
                    if f + 1 < 10:
                        nxt = ldx(f + 1)
                        nmm = mmF(f + 1)
                    mks = [mk + "_%d" % bi for bi in range(4)]
                    sq, sk_ = statr.next()
                    x1t, x1k = x1r.next()
                    xn, xnk = xnr.next()
                    ACT(x1t[0:rows, :], mt_[0:rows, :], AF.Square, mks, [sk_, x1k], accum_out=sq[0:rows, 0:1])
                    TS(sq[0:rows, 1:2], sq[0:rows, 0:1], 1.0 / D, ALU.mult, [sk_], [sk_], s2=EPS, op1=ALU.add)
                    ACT(sq[0:rows, 2:3], sq[0:rows, 1:2], AF.Sqrt, [sk_], [sk_])
                    RECIP(sq[0:rows, 3:4], sq[0:rows, 2:3], [sk_], [sk_])
                    STT(mt_[0:rows, :], mt_[0:rows, :], sq[0:rows, 3:4], gpm[0:rows, :], ALU.mult, ALU.mult,
                        mks + [sk_, "gpm"], mks)
                    TT(x1t[0:rows, :], mt_[0:rows, :], xt[0:rows, :], ALU.add, mks + [xk, x1k], [x1k])
                    s.dma("sp", x1[f * 128:f * 128 + rows, :], x1t[0:rows, :], reads=[x1k])
                    sq2, sk2 = statr.next()
                    ACT(xn[0:rows, :], x1t[0:rows, :], AF.Square, [x1k], [sk2, xnk], accum_out=sq2[0:rows, 0:1])
                    TS(sq2[0:rows, 1:2], sq2[0:rows, 0:1], 1.0 / D, ALU.mult, [sk2], [sk2], s2=EPS, op1=ALU.add)
                    ACT(sq2[0:rows, 2:3], sq2[0:rows, 1:2], AF.Sqrt, [sk2], [sk2])
                    RECIP(sq2[0:rows, 3:4], sq2[0:rows, 2:3], [sk2], [sk2])
                    STT(xn[0:rows, :], x1t[0:rows, :], sq2[0:rows, 3:4], gpf[0:rows, :], ALU.mult, ALU.mult,
                        [x1k, sk2, "gpf", xnk], [xnk])
                    if f + 1 < 10:
                        evF(nmm)
                    transpose_to(xn, xnk, rows, 16, xn2T, "xn2T", f * 128)
                s.flush()
            if nstages < 9:
                return nc
            with ExitStack() as st:
                wur = Ring(st, nc, "wu", 2, [128, 16, 1024], BF16)
                ur = Ring(st, nc, "usb", 4, [128, 1030], F32)
                cr = Ring(st, nc, "csb", 4, [128, 1028], F32)
                yr = Ring(st, nc, "ysb", 2, [128, 1028], BF16)
                UCH = [(126, 512, 0), (638, 512, 512), (1150, 6, 1024)]

                def ldu(q):
                    wt, wk = wur.next()
                    wload(w_up, D, q * 512, 512, wt, wk, col_off=0)
                    wload(w_up, D, DFF + q * 512, 512, wt, wk, col_off=512)
                    return wt, wk

                nxt = ldu(0)
                for i in range(44):
                    if i % 4 == 0:
                        wt, wk = nxt
                        if i // 4 + 1 < 11:
                            nxt = ldu(i // 4 + 1)
                    r4 = i % 4
                    cs_ = []
                    for gi in range(2):
                        ct = gi * 44 + i
                        u, uk = ur.next()
                        for (t0, nt, o0) in UCH:
                            ps, psk = pf.next()
                            MM(ps[:, 0:nt], [(wt[:, kt, gi * 512 + r4 * 128:gi * 512 + (r4 + 1) * 128], xn2T[:, kt, t0:t0 + nt])
                                             for kt in range(16)], [wk, "xn2T"], [psk])
                            CP("act", u[:, o0:o0 + nt], ps[:, 0:nt], [psk], [uk])
                        c_, ck_ = cr.next()
                        ACT(c_[:, 0:1024], u[:, 2:1026], AF.Identity, [uk, "wcT"], [ck_], scale=wcT[:, ct, 2:3], bias=wcT[:, ct, 3:4])
                        STT(c_[:, 0:1024], u[:, 1:1025], wcT[:, ct, 1:2], c_[:, 0:1024], ALU.mult, ALU.add, [uk, "wcT", ck_], [ck_])
                        STT(c_[:, 0:1024], u[:, 0:1024], wcT[:, ct, 0:1], c_[:, 0:1024], ALU.mult, ALU.add, [uk, "wcT", ck_], [ck_])
                        ACT(c_[:, 1024:1028], u[:, 1026:1030], AF.Identity, [uk, "wcT"], [ck_ + "s"], scale=wcT[:, ct, 2:3],
                            bias=wcT[:, ct, 3:4])
                        p3 = pastT[:, ct, :].rearrange("p (j r) -> p j r", r=2)
                        STT(c_[:, 1024:1028], p3[:, :, 1], wcT[:, ct, 1:2], c_[:, 1024:1028], ALU.mult, ALU.add,
                            ["pastT", "wcT", ck_ + "s"], [ck_ + "s"])
                        STT(c_[:, 1024:1028], p3[:, :, 0], wcT[:, ct, 0:1], c_[:, 1024:1028], ALU.mult, ALU.add,
                            ["pastT", "wcT", ck_ + "s"], [ck_ + "s"])
                        CP("dve", u6[:, ct, :], u[:, 1024:1030], [uk], ["u6"])
                        cs_.append((c_, ck_))
                    (cg, cgk), (cv_, cvk) = cs_
                    ACT(cg[:, :], cg[:, :], AF.Gelu_apprx_tanh, [cgk, cgk + "s"], [cgk, cgk + "s"])
                    ysb, ysk = yr.next()
                    TT(ysb[:, :], cg[:, :], cv_[:, :], ALU.mult, [cgk, cgk + "s", cvk, cvk + "s"], [ysk])
                    s.dma("sp", ysc[i * 128:(i + 1) * 128, :], ysb[:, :], reads=[ysk])
                s.flush()
        if nstages < 10:
            return nc
        with ExitStack() as st:
            yT = T(st, "yT", [128, 44, 1028], BF16)
            for k0 in range(0, 44, 11):
                s.dma("sp", yT[:, k0:k0 + 11, :], ysc[k0 * 128:(k0 + 11) * 128, :].rearrange("(kt p) n -> p kt n", p=128),
                      writes=["yT"])
            wdr = Ring(st, nc, "wd", 2, [128, 44, 256], BF16)
            uo = Ring(st, nc, "uo", 3, [6, 512], F32)
            tail_i = [0]

            def tail_step():
                q4 = tail_i[0]
                if q4 >= 22:
                    return
                tail_i[0] += 1
                ps, psk = pf.next()
                TRS([(ps[0:6, j * 128:(j + 1) * 128], u6[:, q4 * 4 + j, :], identf[:, :]) for j in range(4)],
                    ["u6", "identf"], [psk])
                ut, utk = uo.next()
                CP(evac_eng(), ut[:, :], ps[0:6, :], [psk], [utk])
                s.dma("sp", conv_p[:, q4 * 512:(q4 + 1) * 512], ut[0:2, :], reads=[utk])
                s.dma("sp", conv_s[:, 1, q4 * 512:(q4 + 1) * 512], ut[2:6, :], reads=[utk])

            s32 = Ring(st, nc, "s32d", 3, [128, 256], F32)
            nxt = wdr.next()
            wload(w_dn, DFF, 0, 256, nxt[0], nxt[1])
            for bi in range(8):
                wd_, wdk = nxt
                if bi + 1 < 8:
                    nxt = wdr.next()
                    wload(w_dn, DFF, (bi + 1) * 256, 256, nxt[0], nxt[1])
                for f in range(9):
                    rows = 128 if f < 8 else 4
                    ps, psk = pf.next()
                    MM(ps[0:rows, 0:256], [(yT[:, kt, f * 128:f * 128 + rows], wd_[:, kt, :]) for kt in range(44)],
                       [wdk, "yT"], [psk])
                    sb, sbk = s32.next()
                    CP(evac_eng(), sb[0:rows, :], ps[0:rows, 0:256], [psk], [sbk])
                    s.dma("sp", fsc[f * 128:f * 128 + rows, bi * 256:(bi + 1) * 256], sb[0:rows, :], reads=[sbk])
                    tail_step()
            s.flush()
        if nstages < 11:
            return nc
        with ExitStack() as st:
            gpo = T(st, "gpo", [128, D], F32)
            s.dma("sp", gpo[:], g_po[0].partition_broadcast(128), writes=["gpo"])
            fr = Ring(st, nc, "ft", 3, [128, D], F32); xr = Ring(st, nc, "x1h", 3, [128, D], F32)
            junk = T(st, "junkH", [128, D], BF16)

            def ldH(f):
                rows = 128 if f < 8 else 4
                ft, fk = fr.next(); xt, xk = xr.next()
                s.dma("sp", ft[0:rows, :], fsc[f * 128:f * 128 + rows, :], writes=[fk])
                s.dma("sp", xt[0:rows, :], x1[(f + 1) * 128:(f + 1) * 128 + rows, :], writes=[xk])
                return ft, fk, xt, xk

            LH = {0: ldH(0), 1: ldH(1)}
            for f in range(9):
                rows = 128 if f < 8 else 4
                ft, fk, xt, xk = LH.pop(f)
                if f + 2 < 9:
                    LH[f + 2] = ldH(f + 2)
                col, ck_ = rms_col(ft, fk, rows, junk)
                STT(ft[0:rows, :], ft[0:rows, :], col, gpo[0:rows, :], ALU.mult, ALU.mult, [fk, ck_, "gpo"], [fk])
                TT(ft[0:rows, :], ft[0:rows, :], xt[0:rows, :], ALU.add, [fk, xk], [fk])
                s.dma("sp", y_o[f * 128:(f + 1) * 128, :] if f < 8 else ys_o, ft[0:rows, :], reads=[fk])
            s.flush()
    return nc


def _rope_table(pos):
    half = 8
    inv = (np.float32(500000.0) ** (-np.arange(half, dtype=np.float32) * np.float32(2.0) / np.float32(16.0))).astype(np.float32)
    ang = pos.astype(np.float32)[:, None] * inv[None, :]
    return np.concatenate([np.cos(ang), np.sin(ang)], axis=1).astype(np.float32)


def make_in_maps(inp):
    f = lambda a: np.ascontiguousarray(np.asarray(a, dtype=np.float32))
    xp = f(inp["x_prompt"]); xsm = f(inp["x_sample"])
    shared = {
        "identf": np.eye(128, dtype=np.float32),
        "maskT": np.triu(np.ones((64, 64), dtype=np.float32)),
        "g_pre": f(inp["g_pre_mix"]).reshape(1, D), "w_in": f(inp["w_in"])[0], "b_if": f(inp["b_if"]).reshape(1, 8),
        "sinks": f(inp["attn_sinks"]).reshape(1, 32), "w_a": f(inp["w_branch_a"])[0], "w_b": f(inp["w_branch_b"])[0],
        "w_o": f(inp["w_out"])[0], "g_pm": f(inp["g_post_mix"]).reshape(1, D), "g_pf": f(inp["g_pre_ffn"]).reshape(1, D),
        "w_up": f(inp["w_up"])[0], "w_cv": f(inp["w_conv"])[0], "b_cv": f(inp["b_conv"]).reshape(1, 2 * DFF),
        "w_dn": f(inp["w_down"])[0], "g_po": f(inp["g_post_ffn"]).reshape(1, D),
    }
    qi = np.arange(128)[:, None]; kk = np.arange(256)[None, :]
    band = (kk >= qi) & (kk <= qi + 128)
    m_norm = np.where(band, 0.0, -30000.0).astype(np.float32)
    m_first = np.where(band & (kk >= 128), 0.0, -30000.0).astype(np.float32)
    maps = []
    for c in range(8):
        b, hf = c // 2, c % 2
        T0 = hf * 1024
        xw = np.zeros((2048, D), np.float32)
        lo = T0 - 1024
        if lo < 0:
            xw[-lo:] = xp[b, 0:T0 + 1024]
        else:
            xw[:] = xp[b, lo:T0 + 1024]
        pos = np.arange(lo, T0 + 1024)
        valid = (pos >= 0).astype(np.float32)
        ropet = np.concatenate([_rope_table(np.maximum(pos, 0)), _rope_table(np.full(4, 16384))], axis=0)
        sl = slice(4 * c, 4 * c + 4)
        m = dict(shared)
        m.update({
            "xw": xw, "xs": np.ascontiguousarray(xsm[sl, 0, :]),
            "gvalid": np.ascontiguousarray(np.broadcast_to(valid[None, :], (4, 2048))),
            "masks": np.ascontiguousarray(np.stack([m_norm, m_norm if hf == 1 else m_first])),
            "ropet": np.ascontiguousarray(ropet),
            "st_c": f(inp["state_mlstm_c"])[0, sl], "st_n": f(inp["state_mlstm_n"])[0, sl], "st_m": f(inp["state_mlstm_m"])[0, sl],
            "ck": f(inp["cache_swa_k"])[0, sl].reshape(4, 128, 256), "cv": f(inp["cache_swa_v"])[0, sl].reshape(4, 128, 256),
            "cst": f(inp["state_ffn_conv"])[0, sl].reshape(8, 2 * DFF),
        })
        maps.append({k: np.ascontiguousarray(v) for k, v in m.items()})
    return maps


def assemble(res):
    r = res
    yp = np.zeros((4, 2048, D), np.float32); ysm = np.zeros((32, 1, D), np.float32)
    cp = np.zeros((1, 4, 4, 512, 256), np.float32); npp = np.zeros((1, 4, 4, 256), np.float32); mp = np.zeros((1, 4, 4), np.float32)
    kp = np.zeros((1, 4, 128, 4, 64), np.float32); vp = np.zeros((1, 4, 128, 4, 64), np.float32)
    fp = np.zeros((1, 4, 2, 2 * DFF), np.float32)
    cs = np.zeros((1, 32, 4, 512, 256), np.float32); ns = np.zeros((1, 32, 4, 256), np.float32); ms = np.zeros((1, 32, 4), np.float32)
    ks = np.zeros((1, 32, 128, 4, 64), np.float32); vs = np.zeros((1, 32, 128, 4, 64), np.float32)
    fs = np.zeros((1, 32, 2, 2 * DFF), np.float32)
    for c in range(8):
        b, hf = c // 2, c % 2
        o = r[c]
        yp[b, hf * 1024:(hf + 1) * 1024] = o["y"]
        sl = slice(4 * c, 4 * c + 4)
        ysm[sl, 0] = o["ys"]
        if hf == 1:
            cp[0, b] = o["c_p"]; npp[0, b] = o["n_p"]; mp[0, b] = o["m_p"].reshape(4)
            kp[0, b] = o["k_p"].reshape(128, 4, 64); vp[0, b] = o["v_p"].reshape(128, 4, 64); fp[0, b] = o["conv_p"]
        cs[0, sl] = o["c_s"]; ns[0, sl] = o["n_s"]; ms[0, sl] = o["m_s"]
        ks[0, sl] = o["k_s"].reshape(4, 128, 4, 64); vs[0, sl] = o["v_s"].reshape(4, 128, 4, 64); fs[0, sl] = o["conv_s"]
    return (yp, ysm, cp, npp, mp, kp, vp, fp, cs, ns, ms, ks, vs, fs)


def kernel(**inputs):
    nst = int(os.environ.get("KSTAGES", "99"))
    nc = build(nst, False)
    maps = make_in_maps(inputs)
    res = run_bass_kernel_spmd(nc, maps, core_ids=list(range(8)))
    return assemble(res.results)
```
